# Optimizing a Trainium2 kernel written in Bass

```python
import math
import jax, jax.numpy as jnp
from jax import lax
import numpy as np

D_MODEL = 1024
BATCH = 8
SEQ = 4096
DEPTH = 2

PLE_DIM = 256
N_MIXERS = 2
N_GLA_LAYERS = (DEPTH + 1) // 2
N_SWA_LAYERS = DEPTH // 2

GLA_HEADS = 4
GLA_DK = D_MODEL // 2 // GLA_HEADS
GLA_DV = D_MODEL // GLA_HEADS
GLA_LOWRANK = 16
GLA_TAU = 16.0
GLA_CHUNK = 64
GLA_IN_COLS = 2 * GLA_HEADS * GLA_DK + 2 * GLA_HEADS * GLA_DV + GLA_LOWRANK

SWA_HEAD_DIM = 64
SWA_Q_HEADS = D_MODEL // SWA_HEAD_DIM
SWA_KV_HEADS = 4
SWA_WINDOW = 128
SWA_BLOCK = 128
SWA_IN_COLS = (SWA_Q_HEADS + 2 * SWA_KV_HEADS) * SWA_HEAD_DIM

D_FF = 4 * D_MODEL

DEEPNORM_ALPHA = (2.0 * DEPTH) ** 0.25
DEEPNORM_BETA = (8.0 * DEPTH) ** -0.25
LN_EPS = 1e-5
RMS_EPS = 1e-5

kernel_name = 'hybrid_gla_swa_sink_deepnorm'


def layer_norm(x, g, b):
    xf = x.astype(jnp.float32)
    mu = jnp.mean(xf, axis=-1, keepdims=True)
    var = jnp.mean(jnp.square(xf - mu), axis=-1, keepdims=True)
    return ((xf - mu) * lax.rsqrt(var + LN_EPS) * g.astype(jnp.float32) + b.astype(jnp.float32)).astype(x.dtype)


def gla_mixer(x, w_in, w_gk_up, b_gk, norm_g, w_out):
    B, S, _ = x.shape
    H, DK, DV, C = GLA_HEADS, GLA_DK, GLA_DV, GLA_CHUNK
    nc = S // C
    f32 = jnp.float32
    proj = x @ w_in
    q, k, v, r, gk_low = jnp.split(
        proj, [H * DK, 2 * H * DK, 2 * H * DK + H * DV, 2 * H * DK + 2 * H * DV], axis=-1)
    log_a = jax.nn.log_sigmoid((gk_low @ w_gk_up + b_gk).astype(f32)) / GLA_TAU

    def to_chunks(t, d):
        return t.astype(f32).reshape(B, nc, C, H, d).transpose(1, 0, 3, 2, 4)

    qc = to_chunks(q, DK) * (DK ** -0.5)
    kc = to_chunks(k, DK)
    vc = to_chunks(v, DV)
    gc = to_chunks(log_a, DK)
    causal = jnp.tril(jnp.ones((C, C), dtype=bool))[:, :, None]

    def step(state, inp):
        qb, kb, vb, gb = inp
        bcum = jnp.cumsum(gb, axis=2)
        b_last = bcum[:, :, -1:, :]
        o_inter = jnp.einsum('bhcd,bhde->bhce', qb * jnp.exp(bcum), state)
        diff = bcum[:, :, :, None, :] - bcum[:, :, None, :, :]
        decay = jnp.exp(jnp.where(causal, diff, -jnp.inf))
        attn = jnp.einsum('bhid,bhjd,bhijd->bhij', qb, kb, decay)
        o = o_inter + jnp.einsum('bhij,bhje->bhie', attn, vb)
        new_state = jnp.exp(b_last[:, :, 0, :])[..., None] * state + jnp.einsum(
            'bhcd,bhce->bhde', kb * jnp.exp(b_last - bcum), vb)
        return new_state, o

    state0 = jnp.zeros((B, H, DK, DV), f32)
    _, oc = lax.scan(step, state0, (qc, kc, vc, gc))
    o = oc.transpose(1, 0, 3, 2, 4).reshape(B, S, H, DV)
    o = o * lax.rsqrt(jnp.mean(o * o, axis=-1, keepdims=True) + RMS_EPS) * norm_g.astype(f32)
    o = o.reshape(B, S, H * DV) * jax.nn.silu(r.astype(f32))
    return o.astype(x.dtype) @ w_out


def swa_mixer(x, w_qkv, b_qkv, sinks, w_out, b_out):
    B, S, _ = x.shape
    Hq, Hkv, hd, BLK = SWA_Q_HEADS, SWA_KV_HEADS, SWA_HEAD_DIM, SWA_BLOCK
    G = Hq // Hkv
    nb = S // BLK
    f32 = jnp.float32
    qkv = x @ w_qkv + b_qkv
    q, k, v = jnp.split(qkv, [Hq * hd, (Hq + Hkv) * hd], axis=-1)
    q = q.astype(f32).reshape(B, nb, BLK, Hkv, G, hd)
    k = k.astype(f32).reshape(B, nb, BLK, Hkv, hd)
    v = v.astype(f32).reshape(B, nb, BLK, Hkv, hd)
    pad = ((0, 0), (1, 0), (0, 0), (0, 0), (0, 0))
    k_band = jnp.concatenate([jnp.pad(k, pad)[:, :-1], k], axis=2)
    v_band = jnp.concatenate([jnp.pad(v, pad)[:, :-1], v], axis=2)
    scores = jnp.einsum('bnqhgd,bnjhd->bnhgqj', q, k_band) * (hd ** -0.5)
    qi = jnp.arange(BLK)[:, None]
    jj = jnp.arange(2 * BLK)[None, :]
    rel = jj - BLK - qi
    valid = (rel <= 0) & (rel > -SWA_WINDOW)
    scores = jnp.where(valid, scores, -jnp.inf)
    sink = sinks.astype(f32).reshape(Hkv, G)[None, None, :, :, None, None]
    m = jnp.maximum(jnp.max(scores, axis=-1, keepdims=True), sink)
    pr = jnp.exp(scores - m)
    denom = jnp.sum(pr, axis=-1, keepdims=True) + jnp.exp(sink - m)
    out = jnp.einsum('bnhgqj,bnjhd->bnqhgd', pr / denom, v_band)
    out = out.reshape(B, S, Hq * hd).astype(x.dtype)
    return out @ w_out + b_out


def sqrelu_mlp(x, w_up, w_down):
    h = jax.nn.relu(x @ w_up)
    return (h * h) @ w_down


def setup_inputs(seed: int = 0) -> dict:
    key = jax.random.key(seed)
    ks = jax.random.split(key, 22)
    nrm = jax.random.normal
    f32 = jnp.float32
    D = D_MODEL
    x = nrm(ks[0], (BATCH, SEQ, D), f32)
    p = nrm(ks[1], (DEPTH, BATCH, SEQ, PLE_DIM), f32)
    gla_w_in = nrm(ks[2], (N_GLA_LAYERS, D, GLA_IN_COLS), f32) * D ** -0.5
    gla_w_gk_up = nrm(ks[3], (N_GLA_LAYERS, GLA_LOWRANK, GLA_HEADS * GLA_DK), f32) * GLA_LOWRANK ** -0.5
    gla_b_gk = 0.1 * nrm(ks[4], (N_GLA_LAYERS, GLA_HEADS * GLA_DK), f32)
    gla_norm_g = 1.0 + 0.02 * nrm(ks[5], (N_GLA_LAYERS, GLA_DV), f32)
    gla_w_out = nrm(ks[6], (N_GLA_LAYERS, GLA_HEADS * GLA_DV, D), f32) * (GLA_HEADS * GLA_DV) ** -0.5 * DEEPNORM_BETA
    swa_w_qkv = nrm(ks[7], (N_SWA_LAYERS, D, SWA_IN_COLS), f32) * D ** -0.5
    swa_b_qkv = 0.02 * nrm(ks[8], (N_SWA_LAYERS, SWA_IN_COLS), f32)
    swa_sinks = 0.5 * nrm(ks[9], (N_SWA_LAYERS, SWA_Q_HEADS), f32)
    swa_w_out = nrm(ks[10], (N_SWA_LAYERS, SWA_Q_HEADS * SWA_HEAD_DIM, D), f32) * (SWA_Q_HEADS * SWA_HEAD_DIM) ** -0.5 * DEEPNORM_BETA
    swa_b_out = 0.02 * nrm(ks[11], (N_SWA_LAYERS, D), f32)
    mlp_w_up = nrm(ks[12], (DEPTH, D, D_FF), f32) * D ** -0.5
    mlp_w_down = nrm(ks[13], (DEPTH, D_FF, D), f32) * D_FF ** -0.5 * DEEPNORM_BETA
    ln1_g = 1.0 + 0.02 * nrm(ks[14], (DEPTH, D), f32)
    ln1_b = 0.02 * nrm(ks[15], (DEPTH, D), f32)
    ln2_g = 1.0 + 0.02 * nrm(ks[16], (DEPTH, D), f32)
    ln2_b = 0.02 * nrm(ks[17], (DEPTH, D), f32)
    ple_w_proj = nrm(ks[18], (DEPTH, PLE_DIM, D), f32) * PLE_DIM ** -0.5
    ple_w_gate = nrm(ks[19], (DEPTH, D, D), f32) * D ** -0.5
    ple_b_gate = 0.02 * nrm(ks[20], (DEPTH, D), f32)
    return {'x': x, 'p': p,
            'gla_w_in': gla_w_in, 'gla_w_gk_up': gla_w_gk_up, 'gla_b_gk': gla_b_gk,
            'gla_norm_g': gla_norm_g, 'gla_w_out': gla_w_out,
            'swa_w_qkv': swa_w_qkv, 'swa_b_qkv': swa_b_qkv, 'swa_sinks': swa_sinks,
            'swa_w_out': swa_w_out, 'swa_b_out': swa_b_out,
            'mlp_w_up': mlp_w_up, 'mlp_w_down': mlp_w_down,
            'ln1_g': ln1_g, 'ln1_b': ln1_b, 'ln2_g': ln2_g, 'ln2_b': ln2_b,
            'ple_w_proj': ple_w_proj, 'ple_w_gate': ple_w_gate, 'ple_b_gate': ple_b_gate}


def reference(x, p, gla_w_in, gla_w_gk_up, gla_b_gk, gla_norm_g, gla_w_out,
              swa_w_qkv, swa_b_qkv, swa_sinks, swa_w_out, swa_b_out,
              mlp_w_up, mlp_w_down, ln1_g, ln1_b, ln2_g, ln2_b,
              ple_w_proj, ple_w_gate, ple_b_gate):
    h = x
    for i in range(DEPTH):
        j = i // N_MIXERS
        if i % N_MIXERS == 0:
            mix = gla_mixer(h, gla_w_in[j], gla_w_gk_up[j], gla_b_gk[j], gla_norm_g[j], gla_w_out[j])
        else:
            mix = swa_mixer(h, swa_w_qkv[j], swa_b_qkv[j], swa_sinks[j], swa_w_out[j], swa_b_out[j])
        h = layer_norm(DEEPNORM_ALPHA * h + mix, ln1_g[i], ln1_b[i])
        h = layer_norm(DEEPNORM_ALPHA * h + sqrelu_mlp(h, mlp_w_up[i], mlp_w_down[i]), ln2_g[i], ln2_b[i])
        gate = jax.nn.sigmoid(h @ ple_w_gate[i] + ple_b_gate[i])
        h = h + gate * (p[i] @ ple_w_proj[i])
    return h
```

```python
from contextlib import ExitStack
import numpy as np
import concourse.bass as bass
import concourse.mybir as mybir
from concourse.bass_utils import run_bass_kernel_spmd

F32 = mybir.dt.float32
BF16 = mybir.dt.bfloat16
AF = mybir.ActivationFunctionType
ALU = mybir.AluOpType

ENGS = ["pe", "dve", "act", "pool", "sp"]

D = 1024
SEQ = 4096
NB = 8
G = 512
NT = G // 128
UW = 4096
NSLOT = 5
ALPHA = float((2.0 * 2) ** 0.25)
LN_EPS = 1e-5
RMS_EPS = 1e-5
GLA_TAU = 16.0
HALF_LN = -0.6931471805599453


class Op:
    __slots__ = ("eng", "fn", "deps", "dma_key", "token", "signal", "clock", "waits")

    def __init__(self, eng, fn, dma_key):
        self.eng = eng
        self.fn = fn
        self.deps = []
        self.dma_key = dma_key
        self.token = None
        self.signal = False
        self.clock = None
        self.waits = None


class Sched:
    def __init__(self):
        self.ops = {e: [] for e in ENGS}
        self.last_w = {}
        self.readers = {}
        self.order = []

    def op(self, eng, fn, reads=(), writes=(), dma_key=None):
        o = Op(eng, fn, dma_key)
        deps = {}
        for r in reads:
            w = self.last_w.get(r)
            if w is not None:
                deps[id(w)] = w
            if r.startswith("ps"):
                for rd in self.readers.get(r, ()):
                    if rd.eng != eng:
                        deps[id(rd)] = rd
        for r in writes:
            w = self.last_w.get(r)
            if w is not None:
                deps[id(w)] = w
            for rd in self.readers.get(r, ()):
                deps[id(rd)] = rd
        for d in deps.values():
            if d is o:
                continue
            if d.eng == eng and d.dma_key is None and eng == "pe":
                continue
            o.deps.append(d)
        for r in reads:
            self.readers.setdefault(r, []).append(o)
        for r in writes:
            self.last_w[r] = o
            self.readers[r] = []
        self.ops[eng].append(o)
        self.order.append(o)
        return o

    def plan(self):
        for o in self.order:
            for d in o.deps:
                d.signal = True
        counts = {}
        for o in self.order:
            if o.dma_key is not None:
                key = "d_" + str(o.dma_key)
                counts[key] = counts.get(key, 0) + 16
                o.token = (key, counts[key])
            elif o.signal:
                key = "e_" + o.eng
                counts[key] = counts.get(key, 0) + 1
                o.token = (key, counts[key])
        known = {e: {} for e in ENGS}
        nw = 0
        for o in self.order:
            kn = known[o.eng]
            need = {}
            for d in o.deps:
                k, v = d.token
                if kn.get(k, 0) >= v:
                    continue
                if need.get(k, 0) < v:
                    need[k] = v
            final = []
            for k, v in sorted(need.items()):
                implied = False
                for d in o.deps:
                    if d.token[0] == k:
                        continue
                    if d.token[0] in need and d.clock.get(k, 0) >= v:
                        implied = True
                        break
                if not implied:
                    final.append((k, v))
            o.waits = final
            nw += len(final)
            for k, v in need.items():
                if kn.get(k, 0) < v:
                    kn[k] = v
            for d in o.deps:
                for k, v in d.clock.items():
                    if kn.get(k, 0) < v:
                        kn[k] = v
            c = dict(kn)
            if o.token is not None:
                c[o.token[0]] = o.token[1]
            o.clock = c
        self.counts = counts
        self.nwaits = nw
        return sorted(counts.keys())

    def emit_engine(self, eng, h, sems):
        for o in self.ops[eng]:
            for k, v in o.waits:
                h.wait_ge(sems[k], v)
            ins = o.fn(h)
            if o.token is not None:
                k, v = o.token
                ins.then_inc(sems[k], 16 if o.dma_key is not None else 1)


def _fm(W):
    K, M = W.shape
    kc, mc = K // 128, M // 128
    return np.ascontiguousarray(W.reshape(kc, 128, mc, 128).transpose(1, 2, 0, 3)).reshape(128, mc * kc * 128)


def _tm(W, width):
    K, M = W.shape
    kc, nh = K // 128, M // width
    return np.ascontiguousarray(W.reshape(kc, 128, nh, width).transpose(1, 2, 0, 3)).reshape(128, nh * kc * width)


def _pad_unit(A):
    n = A.shape[1]
    tgt = ((n + UW - 1) // UW) * UW
    if tgt == n:
        return A
    out = np.zeros((128, tgt), np.float32)
    out[:, :n] = A
    return out


def _qperm():
    cols = []
    for c in range(8):
        kv, j = c // 2, c % 2
        for hd in (4 * kv + j, 4 * kv + j + 2):
            cols.extend(range(hd * 64, hd * 64 + 64))
    return np.array(cols)


def _kdup():
    cols = []
    for kv in range(4):
        cols.extend(list(range(1024 + kv * 64, 1024 + kv * 64 + 64)) * 2)
    return np.array(cols)


L0_UNITS = dict(q=0, k=1, v=2, r=4, wo=6, up=8, dn=16, pj=24, gt=25)
L0_N = 27
L1_UNITS = dict(q=0, k=2, v=3, wo=4, up=6, dn=14, pj=22, gt=23)
L1_N = 25


def prep_shared(inp):
    f = lambda a: np.asarray(a, np.float32)
    w_in = f(inp["gla_w_in"][0])
    parts = [
        _fm(w_in[:, 0:512]), _fm(w_in[:, 512:1024]),
        _tm(w_in[:, 1024:2048], 512), _tm(w_in[:, 2048:3072], 512),
        _tm(f(inp["gla_w_out"][0]), 512),
        _fm(f(inp["mlp_w_up"][0])), _tm(f(inp["mlp_w_down"][0]), 512),
        _pad_unit(_tm(f(inp["ple_w_proj"][0]), 512)), _tm(f(inp["ple_w_gate"][0]), 512),
    ]
    wq = f(inp["swa_w_qkv"][0])
    parts += [
        _fm(wq[:, _qperm()]), _fm(wq[:, _kdup()]), _pad_unit(_tm(wq[:, 1280:1536], 256)),
        _tm(f(inp["swa_w_out"][0]), 512),
        _fm(f(inp["mlp_w_up"][1])), _tm(f(inp["mlp_w_down"][1]), 512),
        _pad_unit(_tm(f(inp["ple_w_proj"][1]), 512)), _tm(f(inp["ple_w_gate"][1]), 512),
    ]
    ws = np.ascontiguousarray(np.concatenate(parts, axis=1))
    assert ws.shape[1] == (L0_N + L1_N) * UW, ws.shape
    bc = lambda v: np.ascontiguousarray(np.broadcast_to(f(v)[None, :], (128, f(v).shape[0])))
    lnc = np.zeros((2, 128, LNC_W), np.float32)
    for l in range(2):
        for i, nm in enumerate(["ln1_g", "ln1_b", "ln2_g", "ln2_b"]):
            lnc[l, :, i * 1024:(i + 1) * 1024] = bc(inp[nm][l])
            lnc[l, :, LNC_C + i * 8:LNC_C + (i + 1) * 8] = f(inp[nm][l]).reshape(8, 128).T
        lnc[l, :, LNC_SB:LNC_W] = bc(inp["ple_b_gate"][l])
    lnc[1, :, 4 * 1024:5 * 1024] = bc(inp["swa_b_out"][0])
    wgl = np.ascontiguousarray(w_in[:, 3072:3088].reshape(8, 128, 16).transpose(1, 0, 2)).reshape(128, 128)
    wgu = np.ascontiguousarray(f(inp["gla_w_gk_up"][0]))
    bgk = np.ascontiguousarray(f(inp["gla_b_gk"][0]).reshape(4, 128).T)
    gnb = bc(inp["gla_norm_g"][0])
    bqkv = f(inp["swa_b_qkv"][0])
    bq = np.ascontiguousarray(bqkv[_qperm()].reshape(8, 128).T)
    bk = np.ascontiguousarray(bqkv[_kdup()].reshape(4, 128).T)
    bv = bc(bqkv[1280:1536])
    snk = bc(inp["swa_sinks"][0])
    small = np.ascontiguousarray(np.concatenate([bgk, bq, bk, snk, bv, gnb], axis=1))
    return dict(ws=ws, lnc=lnc, wgl=wgl, wgu=wgu, small=small)


SM_BGK, SM_BQ, SM_BK, SM_SNK, SM_BV, SM_GN, SM_W = 0, 4, 12, 16, 32, 288, 544
LNC_C = 5 * 1024
LNC_SB = LNC_C + 32
LNC_W = LNC_SB + 1024


def build_nc(n_groups=8, layers=(0, 1), dbg=False):
    T = n_groups * G
    nc = bass.Bass("TRN2", target_bir_lowering=False)
    x_d = nc.dram_tensor("x", [T, D], F32, kind="ExternalInput").ap()
    p_d = nc.dram_tensor("p", [2, T, 256], F32, kind="ExternalInput").ap()
    ws_d = nc.dram_tensor("ws", [128, (L0_N + L1_N) * UW], F32, kind="ExternalInput").ap()
    lnc_d = nc.dram_tensor("lnc", [2, 128, LNC_W], F32, kind="ExternalInput").ap()
    wgl_d = nc.dram_tensor("wgl", [128, 128], F32, kind="ExternalInput").ap()
    wgu_d = nc.dram_tensor("wgu", [16, 512], F32, kind="ExternalInput").ap()
    sm_d = nc.dram_tensor("small", [128, SM_W], F32, kind="ExternalInput").ap()
    y_d = nc.dram_tensor("y", [T, D], F32, kind="ExternalOutput").ap()
    wsb_d = nc.dram_tensor("wsb", [128, (L0_N + L1_N) * UW], BF16).ap()
    dbg_d = nc.dram_tensor("dbgo", [128, 8192], F32, kind="ExternalOutput").ap() if dbg else None

    S = Sched()
    st = ExitStack()
    with st:
        def sb(name, shape, dt):
            return st.enter_context(nc.sbuf_tensor(name, shape, dt))

        h_tok = sb("h_tok", [128, NT, D], F32)
        x_bf = sb("x_bf", [128, NT, D], BF16)
        xT = sb("xT", [128, 8, G], BF16)
        ogT = sb("ogT", [128, 8, G], BF16)
        ring = [sb(f"ring{i}", [128, UW], BF16) for i in range(NSLOT)]
        lnc_t = sb("lnc_sb", [128, LNC_SB], F32)
        lnc = lnc_t[:, 0:5 * D].rearrange("p (a d) -> p a d", a=5)
        small = sb("small_sb", [128, SM_W], F32)
        wgl = sb("wgl_sb", [128, 8, 16], BF16)
        wgu = sb("wgu_sb", [16, 512], BF16)
        ident = sb("ident", [128, 128], F32)
        identb = sb("identb", [128, 128], BF16)
        xnb = sb("xnb", [128, NT, D], BF16)
        nmr = sb("nmr", [128, NT], F32)
        ones_row = sb("ones_row", [1, 128], BF16)
        bg_row = sb("bg_row", [1, D], BF16)
        mask_le = sb("mask_le", [128, 128], BF16)
        mask_gt = sb("mask_gt", [128, 128], BF16)
        mask2 = sb("mask2", [128, 2, 128], BF16)
        rmask = sb("rmask", [128, G], BF16)
        nbgk = sb("nbgk", [128, 4], F32)
        esink = sb("esink", [128, 16], F32)
        S_f = sb("S_f", [128, 4, 256], F32)
        S_b = [sb(f"S_b{i}", [128, 4, 256], BF16) for i in range(2)]
        p_tok = sb("p_tok", [128, NT, 256], F32)
        pT = sb("pT", [128, 2, G], BF16)
        mv = sb("mv", [128, NT, 2], F32)
        bst = sb("bst", [128, NT, 12], F32)
        lnv = sb("lnv", [128, NT], F32)
        rstd = sb("rstd", [128, NT], F32)
        tmpa = [sb(f"tmpa{i}", [128, G], F32) for i in range(2)]
        tmpb = tmpa
        sgt = [sb(f"sgt{i}", [128, G], F32) for i in range(2)]
        bar = sb("bar", [128, 8], F32)
        wpj_all = sb("wpj_all", [128, 2, 2048], BF16)
        ARENA_F = 4400
        ARENA_B = 16896
        arF = sb("arF", [128, ARENA_F], F32)
        arB = sb("arB", [128, ARENA_B], BF16)
        hid = arB[:, 0:32 * G].rearrange("p (m t) -> p m t", m=32)
        o = 0
        csb = [arF[:, o + i * G:o + (i + 1) * G] for i in range(2)]; o += 2 * G
        Ep = [arF[:, o + i * G:o + (i + 1) * G] for i in range(2)]; o += 2 * G
        Em = [arF[:, o + i * G:o + (i + 1) * G] for i in range(2)]; o += 2 * G
        khT = [arF[:, o + i * 512:o + (i + 1) * 512].rearrange("p (h i) -> p h i", h=4) for i in range(2)]; o += 1024
        ebl = arF[:, o:o + 16].rearrange("p (h t) -> p h t", h=4); o += 16
        ssq = arF[:, o:o + 4]; o += 4
        rinv2 = [arF[:, o + 4 * i:o + 4 * i + 4] for i in range(2)]; o += 8
        lns = arF[:, o:o + 4]; o += 4
        junk = arF[:, o:o + 256]; o += 256
        assert o <= ARENA_F, o
        o = 0
        qtT = arB[:, o:o + 4 * G].rearrange("p (h t) -> p h t", h=4); o += 4 * G
        ktT = arB[:, o:o + 4 * G].rearrange("p (h t) -> p h t", h=4); o += 4 * G
        v_tok = arB[:, o:o + NT * D].rearrange("p (t d) -> p t d", t=NT); o += NT * D
        kh = [arB[:, o + i * 512:o + (i + 1) * 512].rearrange("p (h d) -> p h d", h=4) for i in range(2)]; o += 1024
        Abf = [arB[:, o + i * 512:o + (i + 1) * 512].rearrange("p (h i) -> p h i", h=4) for i in range(2)]; o += 1024
        gkl = arB[0:16, o:o + G]; o += G
        gate = arB[:, o:o + NT * D].rearrange("p (t d) -> p t d", t=NT); o += NT * D
        og = [arB[:, o + i * D:o + (i + 1) * D] for i in range(2)]; o += 2 * D
        assert o <= ARENA_B, o
        o = 0
        den = arF[:, o:o + 16]; o += 16
        rden = arF[:, o:o + 16]; o += 16
        o = 0
        qT = arB[:, o:o + 8 * G].rearrange("p (c t) -> p c t", c=8); o += 8 * G
        PT = [arB[:, o + i * 4096:o + (i + 1) * 4096].rearrange("p (k h i) -> p k h i", k=2, h=16) for i in range(2)]
        o += 2 * 4096
        Pex = [arB[:, o + i * 512:o + (i + 1) * 512] for i in range(4)]; o += 2048
        ao = [arB[:, o + i * D:o + (i + 1) * D] for i in range(2)]; o += 2 * D
        assert o <= ARENA_B, o
        kT = sb("kT", [128, 4, 128 * (NT + 1)], BF16)
        v_aug = sb("v_aug", [128, NT + 1, 4, 65], BF16)

        ps = [st.enter_context(nc.psum_tensor(f"ps{i}", [128, 512], F32)) for i in range(8)]
        psb = [p_[:].bitcast(BF16) for p_ in ps]
        bank_ctr = [0]

        reserved = set()

        def nb():
            while True:
                b = bank_ctr[0] % 8
                bank_ctr[0] += 1
                if b not in reserved:
                    return b

        def reserve(n):
            out = [nb() for _ in range(n)]
            reserved.update(out)
            return out

        def release(bs_):
            for b in bs_:
                reserved.discard(b)

        ecount = [0]

        def pick(*engs):
            ecount[0] += 1
            return engs[ecount[0] % len(engs)]

        arena_res = {"gla": set(), "swa": set(), "mlp": set()}
        cur_phase = [None]

        def R(name):
            arena_res[cur_phase[0]].add(name)
            return name

        def phase(newp):
            old = cur_phase[0]
            cur_phase[0] = newp
            if old is None or old == newp:
                return
            prev = sorted(arena_res[old])
            bb_ = nb()
            S.op("pe", (lambda bb_=bb_: lambda h: h.matmul(ps[bb_][:, 0:128], lhsT=ones_row[0:1, :], rhs=ones_row[0:1, :], start=True, stop=True))(),
                 reads=["ones_row"], writes=prev + ["arena_tok", f"ps{bb_}"])
            arena_res[old] = set()

        def AR(reads):
            return list(reads) + ["arena_tok"]

        seq = []
        for _g in range(n_groups):
            for _l in layers:
                if _l == 0:
                    seq += [(0, 2, UW), (0, 0, UW), (0, 1, UW)] + [(0, u, UW) for u in range(3, 24)] + [(0, 25, UW), (0, 26, UW)]
                else:
                    seq += [(1, 0, UW), (1, 1, UW), (1, 2, UW), (1, 3, 2048)] + [(1, u, UW) for u in range(4, 22)] + [(1, 23, UW), (1, 24, UW)]
        units_per_group = len(seq) // n_groups
        wctr = [0]
        issued = [0]
        PREF = NSLOT - 2

        def issue_next():
            j = issued[0]
            if j >= len(seq):
                return
            issued[0] += 1
            layer, uidx, ncols = seq[j]
            s_ = j % NSLOT
            base = (uidx if layer == 0 else L0_N + uidx) * UW
            ures = f"wsb{layer}_{uidx}"
            if j < units_per_group:
                S.op("pool", (lambda s_=s_, base=base, ncols=ncols: lambda h: h.dma_start(out=ring[s_][:, 0:ncols], in_=ws_d[:, base:base + ncols]))(),
                     writes=[f"ring{s_}"], dma_key=f"ringc{s_}")
                if n_groups > 1:
                    S.op("sp", (lambda s_=s_, base=base, ncols=ncols: lambda h: h.dma_start(out=wsb_d[:, base:base + ncols], in_=ring[s_][:, 0:ncols]))(),
                         reads=[f"ring{s_}"], writes=[ures], dma_key=f"wb{s_}")
            else:
                S.op("sp", (lambda s_=s_, base=base, ncols=ncols: lambda h: h.dma_start(out=ring[s_][:, 0:ncols], in_=wsb_d[:, base:base + ncols]))(),
                     reads=[ures], writes=[f"ring{s_}"], dma_key=f"ring{s_}")

        def wload(layer, uidx, ncols=UW):
            k = wctr[0]
            wctr[0] += 1
            assert seq[k] == (layer, uidx, ncols), (k, seq[k], layer, uidx, ncols)
            while issued[0] <= min(k + PREF, len(seq) - 1):
                issue_next()
            s_ = k % NSLOT
            return ring[s_], f"ring{s_}"

        S.op("sp", lambda h: h.dma_start(out=small[:], in_=sm_d[:, :]), writes=["small"], dma_key="c0")
        S.op("pool", lambda h: h.dma_start(out=wgl[:].rearrange("p k j -> p (k j)"), in_=wgl_d[:, :]), writes=["wgl"], dma_key="c1")
        S.op("pool", lambda h: h.dma_start(out=wgu[:], in_=wgu_d[:, :]), writes=["wgu"], dma_key="c2")
        for _l in range(2):
            _base = ((L0_UNITS["pj"]) if _l == 0 else (L0_N + L1_UNITS["pj"])) * UW
            S.op("pool", (lambda _l=_l, _base=_base: lambda h: h.dma_start(out=wpj_all[:, _l, :], in_=ws_d[:, _base:_base + 2048]))(), writes=["wpj"], dma_key=f"c3{_l}")
        S.op("pool", lambda h: h.memset(ident[:], 1.0), writes=["ident"])
        S.op("pool", lambda h: h.affine_select(out=ident[:], in_=ident[:], pattern=[[-1, 128]], compare_op=ALU.is_equal, fill=0.0, base=0, channel_multiplier=1), reads=["ident"], writes=["ident"])
        S.op("pool", lambda h: h.memset(ones_row[:], 1.0), writes=["ones_row"])
        S.op("pool", lambda h: h.memset(identb[:], 1.0), writes=["identb"])
        S.op("pool", lambda h: h.affine_select(out=identb[:], in_=identb[:], pattern=[[-1, 128]], compare_op=ALU.is_equal, fill=0.0, base=0, channel_multiplier=1), reads=["identb"], writes=["identb"])
        S.op("pool", lambda h: h.memset(mask_le[:], 1.0), writes=["mask_le"])
        S.op("pool", lambda h: h.affine_select(out=mask_le[:], in_=mask_le[:], pattern=[[1, 128]], compare_op=ALU.is_ge, fill=0.0, base=0, channel_multiplier=-1), reads=["mask_le"], writes=["mask_le"])
        S.op("pool", lambda h: h.memset(mask_gt[:], 1.0), writes=["mask_gt"])
        S.op("pool", lambda h: h.affine_select(out=mask_gt[:], in_=mask_gt[:], pattern=[[-1, 128]], compare_op=ALU.is_gt, fill=0.0, base=0, channel_multiplier=1), reads=["mask_gt"], writes=["mask_gt"])
        S.op("pool", lambda h: h.tensor_copy(out=mask2[:, 0, :], in_=mask_gt[:]), reads=["mask_gt"], writes=["mask2"])
        S.op("pool", lambda h: h.tensor_copy(out=mask2[:, 1, :], in_=mask_le[:]), reads=["mask_le", "mask2"], writes=["mask2"])
        S.op("pool", lambda h: h.memset(rmask[:], 1.0), writes=["rmask"])
        for t in range(NT):
            S.op("pool", (lambda t=t: lambda h: h.memset(rmask[:, t * 128:t * 128 + 1], 0.0))(), reads=["rmask"], writes=["rmask"])
        S.op("pool", lambda h: h.memset(S_f[:], 0.0), writes=[f"S_f{hd}" for hd in range(4)])
        S.op("pool", lambda h: h.memset(S_b[0][:], 0.0), writes=[f"S_b0_{hd}" for hd in range(4)])
        S.op("pool", lambda h: h.memset(S_b[1][:], 0.0), writes=[f"S_b1_{hd}" for hd in range(4)])
        S.op("pool", lambda h: h.memset(kT[:], 0.0), writes=["kT"])
        S.op("pool", lambda h: h.memset(v_aug[:], 0.0), writes=["v_aug"])
        S.op("pool", lambda h: h.memset(v_aug[:, :, :, 64:65], 1.0), reads=["v_aug"], writes=["v_aug"])
        S.op("dve", lambda h: h.tensor_scalar(out=nbgk[:], in0=small[:, SM_BGK:SM_BGK + 4], scalar1=-1.0, scalar2=None, op0=ALU.mult), reads=["small"], writes=["nbgk"])
        S.op("act", lambda h: h.activation(out=esink[:], in_=small[:, SM_SNK:SM_SNK + 16], func=AF.Exp), reads=["small"], writes=["esink"])

        def tile_to_T(src, src_res_prefix, t, dst, dst_res, src_is_bf16=False):
            if src_is_bf16:
                stg, stg_res = src, src_res_prefix
            else:
                stg, stg_res = xnb, "xnb"
                e = pick("act", "dve")
                def c_(h, t=t, e=e):
                    if e == "act":
                        return h.copy(out=xnb[:, t, :], in_=src[:, t, :])
                    return h.tensor_copy(out=xnb[:, t, :], in_=src[:, t, :])
                S.op(e, c_, reads=[f"{src_res_prefix}{t}"], writes=[f"xnb{t}"])
            for half in range(2):
                b = nb()
                def f(h, t=t, half=half, b=b, stg=stg):
                    ins = None
                    for j in range(4):
                        kc = half * 4 + j
                        ins = h.transpose(psb[b][:, j * 128:(j + 1) * 128], stg[:, t, kc * 128:(kc + 1) * 128], identb[:])
                    return ins
                S.op("pe", f, reads=[f"{stg_res}{t}", "identb"], writes=[f"ps{b}"])
                e = pick("act", "dve")
                def g(h, t=t, half=half, b=b, e=e):
                    o_ap = dst[:, half * 4:(half + 1) * 4, t * 128:(t + 1) * 128]
                    i_ap = psb[b][:, 0:512].rearrange("p (j i) -> p j i", j=4)
                    if e == "act":
                        return h.copy(out=o_ap, in_=i_ap)
                    return h.tensor_copy(out=o_ap, in_=i_ap)
                S.op(e, g, reads=[f"ps{b}"], writes=[f"{dst_res}{t}"])

        def transposes_to_xT(src, src_res_prefix, dst, dst_res, src_is_bf16=False):
            for t in range(NT):
                tile_to_T(src, src_res_prefix, t, dst, dst_res, src_is_bf16)

        def tm_proj(layer, ubase, src, src_res, evac, nunits_per_half=1, kchunks=8, halves=(0, 1)):
            for c in halves:
                banks = [nb() for _ in range(NT)]
                for uu in range(nunits_per_half):
                    wt, wres = wload(layer, ubase + c * nunits_per_half + uu)
                    for t in range(NT):
                        b = banks[t]
                        def f(h, t=t, b=b, wt=wt, uu=uu):
                            ins = None
                            for kc in range(8):
                                kk = uu * 8 + kc
                                ins = h.matmul(ps[b][:], lhsT=src[:, kk, t * 128:(t + 1) * 128], rhs=wt[:, kc * 512:(kc + 1) * 512],
                                               start=(kk == 0), stop=(kk == kchunks - 1))
                            return ins
                        S.op("pe", f, reads=[wres, f"{src_res}{t}"], writes=[f"ps{b}"])
                for t in range(NT):
                    evac(c, t, banks[t])

        def ln_stats(t):
            for hf in range(2):
                S.op("dve", (lambda t=t, hf=hf: lambda h: h.bn_stats(out=bst[:, t, hf * 6:(hf + 1) * 6], in_=h_tok[:, t, hf * 512:(hf + 1) * 512]))(),
                     reads=[f"h_tok{t}"], writes=[f"bst{t}"])
            S.op("dve", (lambda t=t: lambda h: h.bn_aggr(out=mv[:, t, :], in_=bst[:, t, :]))(), reads=[f"bst{t}"], writes=[f"mv{t}"])
            S.op("act", (lambda t=t: lambda h: h.activation(out=lnv[:, t:t + 1], in_=mv[:, t, 1:2], func=AF.Ln, bias=LN_EPS, scale=1.0))(), reads=[f"mv{t}"], writes=[f"lnv{t}"])
            S.op("act", (lambda t=t: lambda h: h.activation(out=rstd[:, t:t + 1], in_=lnv[:, t:t + 1], func=AF.Exp, scale=-0.5))(), reads=[f"lnv{t}"], writes=[f"rstd{t}"])

        def ln_norm(t):
            S.op("dve", (lambda t=t: lambda h: h.tensor_scalar(out=xnb[:, t, :], in0=h_tok[:, t, :], scalar1=mv[:, t, 0:1], scalar2=rstd[:, t:t + 1],
                                                               op0=ALU.subtract, op1=ALU.mult))(),
                 reads=[f"h_tok{t}", f"mv{t}", f"rstd{t}"], writes=[f"xnb{t}"])
            S.op("dve", (lambda t=t: lambda h: h.tensor_scalar(out=nmr[:, t:t + 1], in0=mv[:, t, 0:1], scalar1=rstd[:, t:t + 1], scalar2=-1.0, op0=ALU.mult, op1=ALU.mult))(),
                 reads=[f"mv{t}", f"rstd{t}"], writes=[f"nmr{t}"])

        def ln_residual(gi, bi):
            for t in range(NT):
                S.op("act", (lambda t=t: lambda h: h.activation(out=h_tok[:, t, :], in_=h_tok[:, t, :], func=AF.Identity, scale=rstd[:, t:t + 1], bias=nmr[:, t:t + 1]))(),
                     reads=[f"h_tok{t}", f"rstd{t}", f"nmr{t}"], writes=[f"h_tok{t}"])
                S.op("pool", (lambda t=t: lambda h: h.tensor_tensor(out=h_tok[:, t, :], in0=h_tok[:, t, :], in1=lnc[:, gi, :], op=ALU.mult))(),
                     reads=[f"h_tok{t}", "lnc"], writes=[f"h_tok{t}"])
                S.op("pool", (lambda t=t: lambda h: h.tensor_tensor(out=h_tok[:, t, :], in0=h_tok[:, t, :], in1=lnc[:, bi, :], op=ALU.add))(),
                     reads=[f"h_tok{t}", "lnc"], writes=[f"h_tok{t}"])

        def ln_hook(c, t):
            if c != 1:
                return
            ln_stats(t)
            if t >= 1:
                ln_norm(t - 1)
            if t == NT - 1:
                ln_norm(t)

        def ln_transpose(gi, bi, dst, dst_res):
            gc0 = LNC_C + gi * 8
            bc0 = LNC_C + bi * 8
            banks = reserve(8)
            for t in range(NT):
                def f(h, t=t, banks=banks):
                    ins = None
                    for kc in range(8):
                        ins = h.transpose(psb[banks[kc]][:, t * 128:(t + 1) * 128], xnb[:, t, kc * 128:(kc + 1) * 128], identb[:])
                    return ins
                S.op("pe", f, reads=[f"xnb{t}", "identb"], writes=[f"ps{b_}" for b_ in banks])
            for kc in range(8):
                e = "act" if kc % 2 == 0 else "dve"
                def g_(h, kc=kc, e=e, banks=banks):
                    o_ap = dst[:, kc, :]
                    i_ap = psb[banks[kc]][:, 0:G]
                    if e == "act":
                        return h.activation(out=o_ap, in_=i_ap, func=AF.Identity, scale=lnc_t[:, gc0 + kc:gc0 + kc + 1], bias=lnc_t[:, bc0 + kc:bc0 + kc + 1])
                    return h.tensor_scalar(out=o_ap, in0=i_ap, scalar1=lnc_t[:, gc0 + kc:gc0 + kc + 1], scalar2=lnc_t[:, bc0 + kc:bc0 + kc + 1], op0=ALU.mult, op1=ALU.add)
                S.op(e, g_, reads=[f"ps{banks[kc]}", "lnc"], writes=[f"{dst_res}{t}" for t in range(NT)])
            release(banks)

        def sigmoid_chain(src_ap, dst_ap, src_res, dst_res, tmp, tmp_res, negscale=True):
            S.op("act", lambda h: h.activation(out=tmp, in_=src_ap, func=AF.Exp, scale=-1.0), reads=src_res, writes=[tmp_res])
            S.op("act", lambda h: h.activation(out=tmp, in_=tmp, func=AF.Ln, bias=1.0, scale=1.0), reads=[tmp_res], writes=[tmp_res])
            S.op("act", lambda h: h.activation(out=dst_ap, in_=tmp, func=AF.Exp, scale=-1.0), reads=[tmp_res], writes=dst_res)

        def mlp_and_ple(layer, U, g, last):
            phase("mlp")
            ln_transpose(0, 1, xT, "xT")
            ln_residual(0, 1)
            for u in range(8):
                wt, wres = wload(layer, U["up"] + u)
                for m in range(4):
                    b = nb()
                    mi = u * 4 + m
                    def f(h, b=b, wt=wt, m=m):
                        ins = None
                        for kc in range(8):
                            ins = h.matmul(ps[b][:], lhsT=wt[:, (m * 8 + kc) * 128:(m * 8 + kc + 1) * 128], rhs=xT[:, kc, :], start=(kc == 0), stop=(kc == 7))
                        return ins
                    S.op("pe", f, reads=[wres] + [f"xT{t}" for t in range(NT)], writes=[f"ps{b}"])
                    i = mi % 2
                    S.op("act", (lambda b=b, i=i: lambda h: h.activation(out=tmpa[i][:], in_=ps[b][:], func=AF.Relu))(), reads=[f"ps{b}"], writes=[f"tmpa{i}"])
                    e = pick("dve", "pool")
                    S.op(e, (lambda i=i, mi=mi: lambda h: h.tensor_tensor(out=hid[:, mi, :], in0=tmpa[i][:], in1=tmpa[i][:], op=ALU.mult))(),
                         reads=AR([f"tmpa{i}"]), writes=[R(f"hid{mi}")])

            def evac_dn(c, t, b):
                S.op("dve", (lambda c=c, t=t, b=b: lambda h: h.scalar_tensor_tensor(out=h_tok[:, t, c * 512:(c + 1) * 512], in0=h_tok[:, t, c * 512:(c + 1) * 512],
                                                                                    scalar=ALPHA, in1=ps[b][:], op0=ALU.mult, op1=ALU.add))(),
                     reads=[f"ps{b}", f"h_tok{t}"], writes=[f"h_tok{t}"])
                ln_hook(c, t)
            tm_proj_hid(layer, U["dn"], evac_dn)
            ln_transpose(2, 3, xT, "xT")
            ln_residual(2, 3)
            if last and g + 1 < n_groups:
                transposes_to_xT(x_bf, "x_bf", ogT, "ogT", src_is_bf16=True)
            for t in range(NT):
                b = nb()
                def f(h, t=t, b=b):
                    ins = None
                    for k2 in range(2):
                        ins = h.transpose(ps[b][:, k2 * 128:(k2 + 1) * 128], p_tok[:, t, k2 * 128:(k2 + 1) * 128], ident[:])
                    return ins
                S.op("pe", f, reads=["p_tok", "ident"], writes=[f"ps{b}"])
                S.op("act", (lambda t=t, b=b: lambda h: h.copy(out=pT[:, :, t * 128:(t + 1) * 128], in_=ps[b][:, 0:256].rearrange("p (k i) -> p k i", k=2)))(),
                     reads=[f"ps{b}"], writes=[f"pT{t}"])
            wpj, wpj_res = wpj_all[:, layer, :], "wpj"

            def evac_gate(c, t, b):
                i = (c * NT + t) % 2
                S.op("act", (lambda b=b, i=i: lambda h: h.activation(out=sgt[i][:], in_=ps[b][:], func=AF.Tanh, scale=0.5))(), reads=[f"ps{b}"], writes=[f"sgt{i}"])
                b2 = nb()
                def f(h, c=c, t=t, b2=b2):
                    ins = None
                    for k2 in range(2):
                        ins = h.matmul(ps[b2][:], lhsT=pT[:, k2, t * 128:(t + 1) * 128], rhs=wpj[:, (c * 2 + k2) * 512:(c * 2 + k2 + 1) * 512], start=(k2 == 0), stop=(k2 == 1))
                    return ins
                S.op("pe", f, reads=[wpj_res, f"pT{t}"], writes=[f"ps{b2}"])
                S.op("dve", (lambda i=i, b2=b2: lambda h: h.scalar_tensor_tensor(out=sgt[i][:], in0=sgt[i][:], scalar=1.0, in1=ps[b2][:], op0=ALU.add, op1=ALU.mult))(),
                     reads=[f"sgt{i}", f"ps{b2}"], writes=[f"sgt{i}"])
                S.op("dve", (lambda c=c, t=t, i=i: lambda h: h.scalar_tensor_tensor(out=h_tok[:, t, c * 512:(c + 1) * 512], in0=sgt[i][:], scalar=0.5, in1=h_tok[:, t, c * 512:(c + 1) * 512], op0=ALU.mult, op1=ALU.add))(),
                     reads=[f"sgt{i}", f"h_tok{t}"], writes=[f"h_tok{t}"])
            wts = [wload(layer, U["gt"]), wload(layer, U["gt"] + 1)]
            for t in range(NT):
                for c in range(2):
                    wt, wres = wts[c]
                    b = nb()
                    def f(h, t=t, b=b, wt=wt, c=c):
                        for kc in range(8):
                            h.matmul(ps[b][:], lhsT=xT[:, kc, t * 128:(t + 1) * 128], rhs=wt[:, kc * 512:(kc + 1) * 512], start=(kc == 0), stop=False)
                        return h.matmul(ps[b][:], lhsT=ones_row[0:1, :], rhs=bg_row[0:1, c * 512:(c + 1) * 512], start=False, stop=True)
                    S.op("pe", f, reads=[wres, f"xT{t}", "ones_row", "bg_row"], writes=[f"ps{b}"])
                    evac_gate(c, t, b)
            if not last:
                for t in range(NT):
                    tile_to_T(h_tok, "h_tok", t, xT, "xT")

        def tm_proj_hid(layer, ubase, evac):
            for c in range(2):
                banks = [nb() for _ in range(NT)]
                for uu in range(4):
                    wt, wres = wload(layer, ubase + c * 4 + uu)
                    for t in range(NT):
                        b = banks[t]
                        def f(h, t=t, b=b, wt=wt, uu=uu):
                            ins = None
                            for kc in range(8):
                                kk = uu * 8 + kc
                                ins = h.matmul(ps[b][:], lhsT=hid[:, kk, t * 128:(t + 1) * 128], rhs=wt[:, kc * 512:(kc + 1) * 512], start=(kk == 0), stop=(kk == 31))
                            return ins
                        S.op("pe", f, reads=AR([wres] + [f"hid{uu * 8 + kc}" for kc in range(8)]), writes=[f"ps{b}"])
                for t in range(NT):
                    evac(c, t, banks[t])

        for g in range(n_groups):
            if g == 0:
                S.op("pool", (lambda g=g: lambda h: h.dma_start(out=x_bf[:], in_=x_d[g * G:(g + 1) * G, :].rearrange("(t p) d -> p t d", p=128)))(),
                     writes=[f"x_bf{t}" for t in range(NT)], dma_key="xbf")
            S.op("pool", (lambda g=g: lambda h: h.dma_start(out=h_tok[:], in_=x_d[g * G:(g + 1) * G, :].rearrange("(t p) d -> p t d", p=128)))(),
                 writes=[f"h_tok{t}" for t in range(NT)], dma_key="xin")
            for layer in layers:
                S.op("pool", (lambda layer=layer: lambda h: h.dma_start(out=lnc_t[:], in_=lnc_d[layer, :, 0:LNC_SB]))(), writes=["lnc"], dma_key="lnc")
                S.op("pool", (lambda layer=layer, g=g: lambda h: h.dma_start(out=p_tok[:], in_=p_d[layer, g * G:(g + 1) * G, :].rearrange("(t p) d -> p t d", p=128)))(),
                     writes=["p_tok"], dma_key="ptok")
                S.op("pool", (lambda layer=layer: lambda h: h.dma_start(out=bg_row[:], in_=lnc_d[layer, 0:1, LNC_SB:LNC_W]))(), writes=["bg_row"], dma_key="bgrow")
                first = (layer == layers[0])
                last = (layer == layers[-1])
                if first and g == 0:
                    transposes_to_xT(x_bf, "x_bf", ogT, "ogT", src_is_bf16=True)
                inT, inres = (ogT, "ogT") if first else (xT, "xT")
                xres = [f"{inres}{t}" for t in range(NT)]
                def evac_wo(c, t, b, layer=layer, first=first):
                    rsrc = h_tok
                    rres = f"h_tok{t}"
                    S.op("dve", (lambda c=c, t=t, b=b, rsrc=rsrc: lambda h: h.scalar_tensor_tensor(out=h_tok[:, t, c * 512:(c + 1) * 512], in0=rsrc[:, t, c * 512:(c + 1) * 512],
                                                                                        scalar=ALPHA, in1=ps[b][:], op0=ALU.mult, op1=ALU.add))(),
                         reads=[f"ps{b}", rres, f"h_tok{t}"], writes=[f"h_tok{t}"])
                    if layer == 1:
                        S.op("dve", (lambda c=c, t=t: lambda h: h.tensor_tensor(out=h_tok[:, t, c * 512:(c + 1) * 512], in0=h_tok[:, t, c * 512:(c + 1) * 512], in1=lnc[:, 4, c * 512:(c + 1) * 512], op=ALU.add))(),
                             reads=[f"h_tok{t}", "lnc"], writes=[f"h_tok{t}"])
                    ln_hook(c, t)
                wo_units = []

                def wo_load(U):
                    wo_units.append(wload(layer, U["wo"]))
                    wo_units.append(wload(layer, U["wo"] + 1))

                def WO(t):
                    for c in range(2):
                        wt, wres = wo_units[c]
                        b = nb()
                        def f(h, t=t, b=b, wt=wt):
                            ins = None
                            for kc in range(8):
                                ins = h.matmul(ps[b][:], lhsT=ogT[:, kc, t * 128:(t + 1) * 128], rhs=wt[:, kc * 512:(kc + 1) * 512], start=(kc == 0), stop=(kc == 7))
                            return ins
                        S.op("pe", f, reads=[wres, f"ogT{t}"], writes=[f"ps{b}"])
                        evac_wo(c, t, b)

                if layer == 0:
                    U = L0_UNITS
                    phase("gla")
                    b = nb()
                    def f(h, b=b, inT=inT):
                        ins = None
                        for kc in range(8):
                            ins = h.matmul(ps[b][0:16, :], lhsT=wgl[:, kc, :], rhs=inT[:, kc, :], start=(kc == 0), stop=(kc == 7))
                        return ins
                    S.op("pe", f, reads=["wgl"] + xres, writes=[f"ps{b}"])
                    S.op("dve", (lambda b=b: lambda h: h.tensor_copy(out=gkl, in_=ps[b][0:16, :]))(), reads=AR([f"ps{b}"]), writes=[R("gkl")])
                    def evac_v(c, t, b):
                        e = pick("act", "dve")
                        def gfn(h, c=c, t=t, b=b, e=e):
                            if e == "act":
                                return h.copy(out=v_tok[:, t, c * 512:(c + 1) * 512], in_=ps[b][:])
                            return h.tensor_copy(out=v_tok[:, t, c * 512:(c + 1) * 512], in_=ps[b][:])
                        S.op(e, gfn, reads=AR([f"ps{b}"]), writes=[R(f"v_tok{t}_{c}")])
                    tm_proj(0, U["v"], inT, inres, evac_v, halves=(0,))
                    wq_t, wq_res = wload(0, U["q"])
                    wk_t, wk_res = wload(0, U["k"])
                    bgs = reserve(4)
                    for hd in range(4):
                        bg = bgs[hd]
                        S.op("pe", (lambda bg=bg, hd=hd: lambda h: h.matmul(ps[bg][:], lhsT=wgu[:, hd * 128:(hd + 1) * 128], rhs=gkl, start=True, stop=True))(),
                             reads=AR(["wgu", "gkl"]), writes=[f"ps{bg}"])

                    def chain(hd):
                        bg = bgs[hd]
                        i = hd % 2
                        S.op("act", (lambda bg=bg, hd=hd, i=i: lambda h: h.activation(out=tmpa[i][:], in_=ps[bg][:], func=AF.Exp, scale=-1.0, bias=nbgk[:, hd:hd + 1]))(),
                             reads=[f"ps{bg}", "nbgk"], writes=[f"tmpa{i}"])
                        release([bg])
                        S.op("act", (lambda i=i: lambda h: h.activation(out=tmpa[i][:], in_=tmpa[i][:], func=AF.Ln, bias=1.0, scale=1.0))(), reads=[f"tmpa{i}"], writes=[f"tmpa{i}"])
                        S.op("dve", (lambda i=i: lambda h: h.tensor_tensor_scan(out=csb[i], data0=rmask[:], data1=tmpa[i][:], initial=0.0, op0=ALU.mult, op1=ALU.add))(),
                             reads=AR([f"tmpa{i}", "rmask"]), writes=[R(f"csb{i}")])
                        S.op("act", (lambda i=i: lambda h: h.activation(out=Ep[i], in_=csb[i], func=AF.Exp, scale=-1.0 / GLA_TAU))(), reads=AR([f"csb{i}"]), writes=[R(f"Ep{i}")])
                        S.op("act", (lambda i=i: lambda h: h.activation(out=Em[i], in_=csb[i], func=AF.Exp, scale=1.0 / GLA_TAU))(), reads=AR([f"csb{i}"]), writes=[R(f"Em{i}")])
                        S.op("pool", (lambda hd=hd, i=i: lambda h: h.tensor_copy(out=ebl[:, hd, :], in_=Ep[i].rearrange("p (t i) -> p t i", t=NT)[:, :, 127]))(),
                             reads=AR([f"Ep{i}"]), writes=[R(f"ebl{hd}")])

                    def qk(hd):
                        i = hd % 2
                        bq_, bk_ = nb(), nb()
                        for (bb, wt, wres) in ((bq_, wq_t, wq_res), (bk_, wk_t, wk_res)):
                            def f(h, bb=bb, wt=wt, hd=hd, inT=inT):
                                ins = None
                                for kc in range(8):
                                    ins = h.matmul(ps[bb][:], lhsT=wt[:, (hd * 8 + kc) * 128:(hd * 8 + kc + 1) * 128], rhs=inT[:, kc, :], start=(kc == 0), stop=(kc == 7))
                                return ins
                            S.op("pe", f, reads=[wres] + xres, writes=[f"ps{bb}"])
                        S.op("dve", (lambda hd=hd, bq_=bq_, i=i: lambda h: h.scalar_tensor_tensor(out=qtT[:, hd, :], in0=ps[bq_][:], scalar=float(128 ** -0.5), in1=Ep[i], op0=ALU.mult, op1=ALU.mult))(),
                             reads=AR([f"ps{bq_}", f"Ep{i}"]), writes=[R(f"qtT{hd}")])
                        S.op("dve", (lambda hd=hd, bk_=bk_, i=i: lambda h: h.tensor_tensor(out=ktT[:, hd, :], in0=ps[bk_][:], in1=Em[i], op=ALU.mult))(),
                             reads=AR([f"ps{bk_}", f"Em{i}"]), writes=[R(f"ktT{hd}")])

                    chain(0); chain(1); qk(0); qk(1); chain(2); qk(2); chain(3); qk(3)
                    tm_proj(0, U["v"], inT, inres, evac_v, halves=(1,))
                    def evac_r(c, t, b):
                        i = (c * NT + t) % 2
                        S.op("act", (lambda b=b, i=i: lambda h: h.activation(out=sgt[i][:], in_=ps[b][:], func=AF.Tanh, scale=0.5))(), reads=[f"ps{b}"], writes=[f"sgt{i}"])
                        S.op("dve", (lambda c=c, t=t, b=b, i=i: lambda h: h.scalar_tensor_tensor(out=sgt[i][:], in0=sgt[i][:], scalar=1.0, in1=ps[b][:], op0=ALU.add, op1=ALU.mult))(),
                             reads=[f"sgt{i}", f"ps{b}"], writes=[f"sgt{i}"])
                        S.op("pool", (lambda c=c, t=t, i=i: lambda h: h.tensor_tensor(out=gate[:, t, c * 512:(c + 1) * 512].rearrange("p (a e) -> p a e", a=2), in0=sgt[i][:].rearrange("p (a e) -> p a e", a=2),
                                                                                 in1=small[:, SM_GN:SM_GN + 256].unsqueeze(1).broadcast_to([128, 2, 256]), op=ALU.mult))(),
                             reads=AR([f"sgt{i}", "small"]), writes=[R(f"gate{t}_{c}")])
                    tm_proj(0, U["r"], inT, inres, evac_r)

                    def stage_AB(t):
                        ts = slice(t * 128, (t + 1) * 128)
                        pi = t % 2
                        ba = nb()
                        def f(h, ba=ba, ts=ts):
                            ins = None
                            for hd in range(4):
                                ins = h.matmul(ps[ba][:, hd * 128:(hd + 1) * 128], lhsT=ktT[:, hd, ts], rhs=qtT[:, hd, ts], start=True, stop=True)
                            return ins
                        S.op("pe", f, reads=AR([f"ktT{hd}" for hd in range(4)] + [f"qtT{hd}" for hd in range(4)]), writes=[f"ps{ba}"])
                        S.op("dve", (lambda ba=ba, pi=pi: lambda h: h.tensor_tensor(out=Abf[pi], in0=ps[ba][:].rearrange("p (h i) -> p h i", h=4),
                                                                                 in1=mask_le[:].unsqueeze(1).broadcast_to([128, 4, 128]), op=ALU.mult))(),
                             reads=AR([f"ps{ba}", "mask_le"]), writes=[R(f"Abf{pi}")])
                        for hd in range(4):
                            S.op("act", (lambda pi=pi, ts=ts, t=t, hd=hd: lambda h: h.activation(out=khT[pi][:, hd, :], in_=ktT[:, hd, ts], func=AF.Copy, scale=ebl[:, hd, t:t + 1]))(),
                                 reads=AR([f"ktT{hd}", f"ebl{hd}"]), writes=[R(f"khT{pi}_{hd}")])
                        bt = nb()
                        def f2(h, bt=bt, pi=pi):
                            ins = None
                            for hd in range(4):
                                ins = h.transpose(ps[bt][:, hd * 128:(hd + 1) * 128], khT[pi][:, hd, :], ident[:])
                            return ins
                        S.op("pe", f2, reads=AR([f"khT{pi}_{hd}" for hd in range(4)] + ["ident"]), writes=[f"ps{bt}"])
                        S.op("act", (lambda bt=bt, pi=pi: lambda h: h.copy(out=kh[pi], in_=ps[bt][:].rearrange("p (h d) -> p h d", h=4)))(), reads=AR([f"ps{bt}"]), writes=[R(f"kh{pi}")])

                    pend = {}

                    def stage_D(t):
                        tg = g * NT + t
                        pi = t % 2
                        sb_w = S_b[(tg + 1) % 2]
                        sbw_res = f"S_b{(tg + 1) % 2}"
                        for half in range(2):
                            bs = nb()
                            def f(h, half=half, bs=bs, t=t, pi=pi):
                                ins = None
                                for q_ in range(2):
                                    hd = half * 2 + q_
                                    ins = h.matmul(ps[bs][:, q_ * 256:(q_ + 1) * 256], lhsT=kh[pi][:, hd, :], rhs=v_tok[:, t, hd * 256:(hd + 1) * 256], start=True, stop=True)
                                return ins
                            S.op("pe", f, reads=AR([f"kh{pi}", f"v_tok{t}_{half}"]), writes=[f"ps{bs}"])
                            for q_ in range(2):
                                hd = half * 2 + q_
                                S.op("dve", (lambda bs=bs, hd=hd, t=t, q_=q_, sb_w=sb_w: lambda h: h.scalar_tensor_tensor(out=sb_w[:, hd, :], in0=S_f[:, hd, :], scalar=ebl[:, hd, t:t + 1], in1=ps[bs][:, q_ * 256:(q_ + 1) * 256], op0=ALU.mult, op1=ALU.add))(),
                                     reads=AR([f"ps{bs}", f"ebl{hd}", f"S_f{hd}"]), writes=[f"{sbw_res}_{hd}"])
                            for q_ in range(2):
                                hd = half * 2 + q_
                                S.op("dve", (lambda bs=bs, hd=hd, t=t, q_=q_: lambda h: h.scalar_tensor_tensor(out=S_f[:, hd, :], in0=S_f[:, hd, :], scalar=ebl[:, hd, t:t + 1], in1=ps[bs][:, q_ * 256:(q_ + 1) * 256], op0=ALU.mult, op1=ALU.add))(),
                                     reads=AR([f"ps{bs}", f"ebl{hd}", f"S_f{hd}"]), writes=[f"S_f{hd}"])

                    def stage_C(t):
                        tg = g * NT + t
                        ts = slice(t * 128, (t + 1) * 128)
                        pi = t % 2
                        sb_r = S_b[tg % 2]
                        sbr_res = f"S_b{tg % 2}"
                        bo = reserve(2)
                        for half in range(2):
                            def f(h, half=half, bo=bo, ts=ts, t=t, sb_r=sb_r, pi=pi):
                                ins = None
                                for q_ in range(2):
                                    hd = half * 2 + q_
                                    cs = slice(q_ * 256, q_ * 256 + 256)
                                    h.matmul(ps[bo[half]][:, cs], lhsT=Abf[pi][:, hd, :], rhs=v_tok[:, t, hd * 256:(hd + 1) * 256], start=True, stop=False)
                                    ins = h.matmul(ps[bo[half]][:, cs], lhsT=qtT[:, hd, ts], rhs=sb_r[:, hd, :], start=False, stop=True)
                                return ins
                            S.op("pe", f, reads=AR([f"qtT{half * 2}", f"qtT{half * 2 + 1}", f"{sbr_res}_{half * 2}", f"{sbr_res}_{half * 2 + 1}", f"Abf{pi}", f"v_tok{t}_{half}"]), writes=[f"ps{bo[half]}"])
                        oi = t % 2
                        for hd in range(4):
                            bb = bo[hd // 2]
                            cs = slice((hd % 2) * 256, (hd % 2) * 256 + 256)
                            S.op("act", (lambda bb=bb, cs=cs, hd=hd: lambda h: h.activation(out=junk, in_=ps[bb][:, cs], func=AF.Square, accum_out=ssq[:, hd:hd + 1]))(),
                                 reads=AR([f"ps{bb}"]), writes=[R("junk"), R(f"ssq{hd}")])
                        S.op("act", lambda h: h.activation(out=lns, in_=ssq, func=AF.Ln, bias=RMS_EPS, scale=1.0 / 256.0), reads=AR([f"ssq{hd}" for hd in range(4)]), writes=[R("lns")])
                        S.op("act", (lambda t=t: lambda h: h.activation(out=rinv2[t % 2], in_=lns, func=AF.Exp, scale=-0.5, bias=HALF_LN))(), reads=AR(["lns"]), writes=[R(f"rinv{t % 2}")])
                        pend_og[t] = bo

                    def stage_Cog(t):
                        bo = pend_og.pop(t)
                        oi = t % 2
                        ri = t % 2
                        for hd in range(4):
                            bb = bo[hd // 2]
                            cs = slice((hd % 2) * 256, (hd % 2) * 256 + 256)
                            S.op("dve", (lambda bb=bb, cs=cs, hd=hd, t=t, oi=oi, ri=ri: lambda h: h.scalar_tensor_tensor(out=og[oi][:, hd * 256:(hd + 1) * 256], in0=ps[bb][:, cs], scalar=rinv2[ri][:, hd:hd + 1],
                                                                                                              in1=gate[:, t, hd * 256:(hd + 1) * 256], op0=ALU.mult, op1=ALU.mult))(),
                                 reads=AR([f"ps{bb}", f"rinv{ri}", f"gate{t}_{hd // 2}"]), writes=[R(f"og{oi}_{hd}")])
                        release(bo)

                    def stage_E(t):
                        ts = slice(t * 128, (t + 1) * 128)
                        oi = t % 2
                        for half in range(2):
                            b = nb()
                            def f(h, half=half, b=b, oi=oi):
                                ins = None
                                for j in range(4):
                                    kc = half * 4 + j
                                    ins = h.transpose(psb[b][:, j * 128:(j + 1) * 128], og[oi][:, kc * 128:(kc + 1) * 128], identb[:])
                                return ins
                            S.op("pe", f, reads=AR([f"og{oi}_{hd}" for hd in range(4)] + ["identb"]), writes=[f"ps{b}"])
                            e = pick("act", "dve")
                            def ev(h, half=half, b=b, ts=ts, e=e):
                                o_ap = ogT[:, half * 4:(half + 1) * 4, ts]
                                i_ap = psb[b][:, 0:512].rearrange("p (j i) -> p j i", j=4)
                                if e == "act":
                                    return h.copy(out=o_ap, in_=i_ap)
                                return h.tensor_copy(out=o_ap, in_=i_ap)
                            S.op(e, ev, reads=[f"ps{b}"], writes=[f"ogT{t}"])

                    wo_load(U)
                    pend_og = {}
                    stage_AB(0)
                    stage_AB(1)
                    for t in range(NT):
                        stage_D(t)
                        stage_C(t)
                        if t + 2 < NT:
                            stage_AB(t + 2)
                        if t >= 1:
                            stage_Cog(t - 1)
                            stage_E(t - 1)
                    stage_Cog(NT - 1)
                    stage_E(NT - 1)
                    for t in range(NT):
                        WO(t)
                    if dbg and g == 0:
                        dumps = [(csb[1], 0, 512), (Ep[1], 512, 512), (qtT[:, 3, :], 1024, 512), (ktT[:, 3, :], 1536, 512), (v_tok[:, 0, :], 2048, 1024),
                                 (gate[:, 0, :], 3072, 1024), (og[1], 4096, 1024), (ogT[:, 0:2, :], 5120, 1024), (S_f[:], 6144, 1024), (og[0], 7168, 1024)]
                        allres = sorted(arena_res["gla"]) + [f"ogT{t}" for t in range(NT)] + [f"S_f{hd}" for hd in range(4)]
                        for di, (ap_, off, n) in enumerate(dumps):
                            o_ap = dbg_d[:, off:off + n]
                            if len(ap_.shape) == 3:
                                o_ap = o_ap.rearrange("p (a b) -> p a b", a=ap_.shape[1])
                            S.op("pool", (lambda ap_=ap_, o_ap=o_ap: lambda h: h.dma_start(out=o_ap, in_=ap_))(), reads=AR(allres), writes=["dbg_out"], dma_key=f"dbg{di}")
                else:
                    U = L1_UNITS
                    phase("swa")
                    for uq in range(2):
                        wt, wres = wload(1, U["q"] + uq)
                        for m in range(4):
                            c = uq * 4 + m
                            b = nb()
                            def f(h, b=b, wt=wt, m=m, inT=inT):
                                ins = None
                                for kc in range(8):
                                    ins = h.matmul(ps[b][:], lhsT=wt[:, (m * 8 + kc) * 128:(m * 8 + kc + 1) * 128], rhs=inT[:, kc, :], start=(kc == 0), stop=(kc == 7))
                                return ins
                            S.op("pe", f, reads=[wres] + xres, writes=[f"ps{b}"])
                            S.op("dve", (lambda b=b, c=c: lambda h: h.tensor_scalar(out=qT[:, c, :], in0=ps[b][:], scalar1=small[:, SM_BQ + c:SM_BQ + c + 1], scalar2=0.125, op0=ALU.add, op1=ALU.mult))(),
                                 reads=AR([f"ps{b}", "small"]), writes=[R(f"qT{c}")])
                    wt, wres = wload(1, U["k"])
                    for kv in range(4):
                        b = nb()
                        def f(h, b=b, wt=wt, kv=kv, inT=inT):
                            ins = None
                            for kc in range(8):
                                ins = h.matmul(ps[b][:], lhsT=wt[:, (kv * 8 + kc) * 128:(kv * 8 + kc + 1) * 128], rhs=inT[:, kc, :], start=(kc == 0), stop=(kc == 7))
                            return ins
                        S.op("pe", f, reads=[wres] + xres, writes=[f"ps{b}"])
                        S.op("act", (lambda b=b, kv=kv: lambda h: h.activation(out=kT[:, kv, 128:128 + G], in_=ps[b][:], func=AF.Identity, bias=small[:, SM_BK + kv:SM_BK + kv + 1], scale=1.0))(),
                             reads=[f"ps{b}", "small"], writes=[f"kT{kv}"])
                    wt, wres = wload(1, U["v"], ncols=2048)
                    for t in range(NT):
                        b = nb()
                        def f(h, b=b, wt=wt, t=t, inT=inT):
                            ins = None
                            for kc in range(8):
                                ins = h.matmul(ps[b][:, 0:256], lhsT=inT[:, kc, t * 128:(t + 1) * 128], rhs=wt[:, kc * 256:(kc + 1) * 256], start=(kc == 0), stop=(kc == 7))
                            return ins
                        S.op("pe", f, reads=[wres, f"{inres}{t}"], writes=[f"ps{b}"])
                        S.op("dve", (lambda b=b, t=t: lambda h: h.tensor_tensor(out=v_aug[:, t + 1, :, 0:64], in0=ps[b][:, 0:256].rearrange("p (k d) -> p k d", k=4),
                                                                             in1=small[:, SM_BV:SM_BV + 256].rearrange("p (k d) -> p k d", k=4), op=ALU.add))(),
                             reads=[f"ps{b}", "small"], writes=[f"v_aug{t + 1}"])
                    def rec_S(t, kv):
                        pi = t % 2
                        PTt = PT[pi]
                        blo, bhi = nb(), nb()
                        def f(h, blo=blo, bhi=bhi, kv=kv, t=t):
                            ins = None
                            for kt in range(2):
                                kcols = slice((t + kt) * 128, (t + kt + 1) * 128)
                                h.matmul(ps[blo][:, kt * 256:(kt + 1) * 256], lhsT=kT[0:64, kv, kcols], rhs=qT[0:64, 2 * kv:2 * kv + 2, t * 128:(t + 1) * 128], start=True, stop=True)
                                ins = h.matmul(ps[bhi][:, kt * 256:(kt + 1) * 256], lhsT=kT[64:128, kv, kcols], rhs=qT[64:128, 2 * kv:2 * kv + 2, t * 128:(t + 1) * 128], start=True, stop=True)
                            return ins
                        S.op("pe", f, reads=AR([f"kT{kv}", f"qT{2 * kv}", f"qT{2 * kv + 1}"]), writes=[f"ps{blo}", f"ps{bhi}"])
                        for hi, bb in enumerate((blo, bhi)):
                            xi = (kv * 2 + hi) % 4
                            S.op("act", (lambda bb=bb, xi=xi: lambda h: h.activation(out=Pex[xi], in_=ps[bb][:], func=AF.Exp))(), reads=AR([f"ps{bb}"]), writes=[R(f"Pex{xi}")])
                            e = pick("dve", "pool")
                            S.op(e, (lambda xi=xi, kv=kv, hi=hi, PTt=PTt: lambda h: h.tensor_tensor(
                                out=PTt[:, :, kv * 4 + hi * 2:kv * 4 + hi * 2 + 2, :], in0=Pex[xi].rearrange("p (k g i) -> p k g i", k=2, g=2),
                                in1=mask2[:].unsqueeze(2).broadcast_to([128, 2, 2, 128]), op=ALU.mult))(),
                                 reads=AR([f"Pex{xi}", "mask2"]), writes=[R(f"PT{pi}_{kv}_{hi}")])

                    def rec_PV(t, kv):
                        pi = t % 2
                        oi = t % 2
                        PTt = PT[pi]
                        b = nb()
                        def f(h, b=b, kv=kv, t=t, PTt=PTt):
                            ins = None
                            for gq in range(4):
                                hdx = kv * 4 + gq
                                h.matmul(ps[b][:, gq * 65:(gq + 1) * 65], lhsT=PTt[:, 0, hdx, :], rhs=v_aug[:, t, kv, :], start=True, stop=False)
                                ins = h.matmul(ps[b][:, gq * 65:(gq + 1) * 65], lhsT=PTt[:, 1, hdx, :], rhs=v_aug[:, t + 1, kv, :], start=False, stop=True)
                            return ins
                        S.op("pe", f, reads=AR([f"PT{pi}_{kv}_0", f"PT{pi}_{kv}_1", f"v_aug{t}", f"v_aug{t + 1}"]), writes=[f"ps{b}"])
                        pv = ps[b][:, 0:260].rearrange("p (g d) -> p g d", g=4)
                        S.op("dve", (lambda b=b, kv=kv, pv=pv: lambda h: h.tensor_tensor(out=den[:, kv * 4:(kv + 1) * 4], in0=pv[:, :, 64], in1=esink[:, kv * 4:(kv + 1) * 4], op=ALU.add))(),
                             reads=AR([f"ps{b}", "esink"]), writes=[R(f"den{kv}")])
                        S.op("dve", (lambda kv=kv: lambda h: h.reciprocal(out=rden[:, kv * 4:(kv + 1) * 4], in_=den[:, kv * 4:(kv + 1) * 4]))(), reads=AR([f"den{kv}"]), writes=[R(f"rden{kv}")])
                        for gq in range(4):
                            hdx = kv * 4 + gq
                            S.op("dve", (lambda b=b, gq=gq, hdx=hdx, oi=oi: lambda h: h.tensor_scalar(out=ao[oi][:, hdx * 64:(hdx + 1) * 64], in0=ps[b][:, gq * 65:gq * 65 + 64], scalar1=rden[:, hdx:hdx + 1], scalar2=None, op0=ALU.mult))(),
                                 reads=AR([f"ps{b}", f"rden{kv}"]), writes=[R(f"ao{oi}_{hdx}")])

                    def rec_T(t):
                        oi = t % 2
                        ts = slice(t * 128, (t + 1) * 128)
                        for half in range(2):
                            b = nb()
                            def f(h, half=half, b=b, oi=oi):
                                ins = None
                                for j in range(4):
                                    kc = half * 4 + j
                                    ins = h.transpose(psb[b][:, j * 128:(j + 1) * 128], ao[oi][:, kc * 128:(kc + 1) * 128], identb[:])
                                return ins
                            S.op("pe", f, reads=AR([f"ao{oi}_{hdx}" for hdx in range(16)] + ["identb"]), writes=[f"ps{b}"])
                            S.op("dve", (lambda half=half, b=b, ts=ts: lambda h: h.tensor_copy(out=ogT[:, half * 4:(half + 1) * 4, ts], in_=psb[b][:, 0:512].rearrange("p (j i) -> p j i", j=4)))(),
                                 reads=[f"ps{b}"], writes=[f"ogT{t}"])

                    wo_load(U)
                    units = [(t, kv) for t in range(NT) for kv in range(4)]
                    LAG, TL = 3, 2
                    for i in range(len(units) + LAG + TL):
                        if i < len(units):
                            rec_S(*units[i])
                        j = i - LAG
                        if 0 <= j < len(units):
                            rec_PV(*units[j])
                        k = i - LAG - TL
                        if 0 <= k < len(units) and units[k][1] == 3:
                            rec_T(units[k][0])
                    for t in range(NT):
                        WO(t)
                    S.op("pool", lambda h: h.tensor_copy(out=kT[:, :, 0:128], in_=kT[:, :, G:G + 128]), reads=[f"kT{kv}" for kv in range(4)], writes=[f"kT{kv}" for kv in range(4)])
                    S.op("pool", lambda h: h.tensor_copy(out=v_aug[:, 0, :, :], in_=v_aug[:, NT, :, :]), reads=[f"v_aug{NT}", "v_aug0"], writes=["v_aug0"])

                if first and g + 1 < n_groups:
                    S.op("pool", (lambda g=g: lambda h: h.dma_start(out=x_bf[:], in_=x_d[(g + 1) * G:(g + 2) * G, :].rearrange("(t p) d -> p t d", p=128)))(),
                         writes=[f"x_bf{t}" for t in range(NT)], dma_key="xbf")
                mlp_and_ple(layer, U, g, last)
            S.op("pool", (lambda g=g: lambda h: h.dma_start(out=y_d[g * G:(g + 1) * G, :].rearrange("(t p) d -> p t d", p=128), in_=h_tok[:]))(),
                 reads=[f"h_tok{t}" for t in range(NT)], writes=["y_out"], dma_key="yout")
        S.op("sp", lambda h: None, reads=["y_out", "dbg_out"])

        keys = S.plan()
        sems = {k: st.enter_context(nc.semaphore(k)) for k in keys}
        if dbg:
            print("sems", len(keys), "waits", S.nwaits, {e: len(S.ops[e]) for e in ENGS}, S.counts)
        with nc.Block() as block:
            @block.tensor
            def _(h):
                S.emit_engine("pe", h, sems)

            @block.vector
            def _(h):
                S.emit_engine("dve", h, sems)

            @block.scalar
            def _(h):
                S.emit_engine("act", h, sems)

            @block.gpsimd
            def _(h):
                S.emit_engine("pool", h, sems)

            @block.sync
            def _(h):
                S.emit_engine("sp", h, sems)
    return nc


def kernel(**inputs):
    shared = prep_shared(inputs)
    x = np.asarray(inputs["x"], np.float32)
    p = np.asarray(inputs["p"], np.float32)
    nc = build_nc(SEQ // G, (0, 1))
    in_maps = []
    for c in range(NB):
        m = dict(shared)
        m["x"] = np.ascontiguousarray(x[c])
        m["p"] = np.ascontiguousarray(p[:, c])
        in_maps.append(m)
    res = run_bass_kernel_spmd(nc, in_maps, core_ids=list(range(NB)))
    return np.stack([r["y"] for r in res.results], axis=0).astype(np.float32)
```

```python
from contextlib import ExitStack
import numpy as np
import concourse.bass as bass
import concourse.mybir as mybir
from concourse.bass_utils import run_bass_kernel_spmd

F32 = mybir.dt.float32
BF16 = mybir.dt.bfloat16
AF = mybir.ActivationFunctionType
ALU = mybir.AluOpType

ENGS = ["pe", "dve", "act", "pool", "sp"]

D = 1024
SEQ = 4096
NB = 8
G = 512
NT = G // 128
UW = 4096
NSLOT = 5
ALPHA = float((2.0 * 2) ** 0.25)
LN_EPS = 1e-5
RMS_EPS = 1e-5
GLA_TAU = 16.0
HALF_LN = -0.6931471805599453


class Op:
    __slots__ = ("eng", "fn", "deps", "dma_key", "token", "signal", "clock", "waits")

    def __init__(self, eng, fn, dma_key):
        self.eng = eng
        self.fn = fn
        self.deps = []
        self.dma_key = dma_key
        self.token = None
        self.signal = False
        self.clock = None
        self.waits = None


class Sched:
    def __init__(self):
        self.ops = {e: [] for e in ENGS}
        self.last_w = {}
        self.readers = {}
        self.order = []

    def op(self, eng, fn, reads=(), writes=(), dma_key=None):
        o = Op(eng, fn, dma_key)
        deps = {}
        for r in reads:
            w = self.last_w.get(r)
            if w is not None:
                deps[id(w)] = w
            if r.startswith("ps"):
                for rd in self.readers.get(r, ()):
                    if rd.eng != eng:
                        deps[id(rd)] = rd
        for r in writes:
            w = self.last_w.get(r)
            if w is not None:
                deps[id(w)] = w
            for rd in self.readers.get(r, ()):
                deps[id(rd)] = rd
        for d in deps.values():
            if d is o:
                continue
            if d.eng == eng and d.dma_key is None and eng == "pe":
                continue
            o.deps.append(d)
        for r in reads:
            self.readers.setdefault(r, []).append(o)
        for r in writes:
            self.last_w[r] = o
            self.readers[r] = []
        self.ops[eng].append(o)
        self.order.append(o)
        return o

    def plan(self):
        for o in self.order:
            for d in o.deps:
                d.signal = True
        counts = {}
        for o in self.order:
            if o.dma_key is not None:
                key = "d_" + str(o.dma_key)
                counts[key] = counts.get(key, 0) + 16
                o.token = (key, counts[key])
            elif o.signal:
                key = "e_" + o.eng
                counts[key] = counts.get(key, 0) + 1
                o.token = (key, counts[key])
        known = {e: {} for e in ENGS}
        nw = 0
        for o in self.order:
            kn = known[o.eng]
            need = {}
            for d in o.deps:
                k, v = d.token
                if kn.get(k, 0) >= v:
                    continue
                if need.get(k, 0) < v:
                    need[k] = v
            final = []
            for k, v in sorted(need.items()):
                implied = False
                for d in o.deps:
                    if d.token[0] == k:
                        continue
                    if d.token[0] in need and d.clock.get(k, 0) >= v:
                        implied = True
                        break
                if not implied:
                    final.append((k, v))
            o.waits = final
            nw += len(final)
            for k, v in need.items():
                if kn.get(k, 0) < v:
                    kn[k] = v
            for d in o.deps:
                for k, v in d.clock.items():
                    if kn.get(k, 0) < v:
                        kn[k] = v
            c = dict(kn)
            if o.token is not None:
                c[o.token[0]] = o.token[1]
            o.clock = c
        self.counts = counts
        self.nwaits = nw
        return sorted(counts.keys())

    def emit_engine(self, eng, h, sems):
        for o in self.ops[eng]:
            for k, v in o.waits:
                h.wait_ge(sems[k], v)
            ins = o.fn(h)
            if o.token is not None:
                k, v = o.token
                ins.then_inc(sems[k], 16 if o.dma_key is not None else 1)


def _fm(W):
    K, M = W.shape
    kc, mc = K // 128, M // 128
    return np.ascontiguousarray(W.reshape(kc, 128, mc, 128).transpose(1, 2, 0, 3)).reshape(128, mc * kc * 128)


def _tm(W, width):
    K, M = W.shape
    kc, nh = K // 128, M // width
    return np.ascontiguousarray(W.reshape(kc, 128, nh, width).transpose(1, 2, 0, 3)).reshape(128, nh * kc * width)


def _pad_unit(A):
    n = A.shape[1]
    tgt = ((n + UW - 1) // UW) * UW
    if tgt == n:
        return A
    out = np.zeros((128, tgt), np.float32)
    out[:, :n] = A
    return out


def _qperm():
    cols = []
    for c in range(8):
        kv, j = c // 2, c % 2
        for hd in (4 * kv + j, 4 * kv + j + 2):
            cols.extend(range(hd * 64, hd * 64 + 64))
    return np.array(cols)


def _kdup():
    cols = []
    for kv in range(4):
        cols.extend(list(range(1024 + kv * 64, 1024 + kv * 64 + 64)) * 2)
    return np.array(cols)


L0_UNITS = dict(q=0, k=1, v=2, r=4, wo=6, up=8, dn=16, pj=24, gt=25)
L0_N = 27
L1_UNITS = dict(q=0, k=2, v=3, wo=4, up=6, dn=14, pj=22, gt=23)
L1_N = 25


def prep_shared(inp):
    f = lambda a: np.asarray(a, np.float32)
    w_in = f(inp["gla_w_in"][0])
    parts = [
        _fm(w_in[:, 0:512]), _fm(w_in[:, 512:1024]),
        _tm(w_in[:, 1024:2048], 512), _tm(w_in[:, 2048:3072], 512),
        _tm(f(inp["gla_w_out"][0]), 512),
        _fm(f(inp["mlp_w_up"][0])), _tm(f(inp["mlp_w_down"][0]), 512),
        _pad_unit(_tm(f(inp["ple_w_proj"][0]), 512)), _tm(f(inp["ple_w_gate"][0]), 512),
    ]
    wq = f(inp["swa_w_qkv"][0])
    parts += [
        _fm(wq[:, _qperm()]), _fm(wq[:, _kdup()]), _pad_unit(_tm(wq[:, 1280:1536], 256)),
        _tm(f(inp["swa_w_out"][0]), 512),
        _fm(f(inp["mlp_w_up"][1])), _tm(f(inp["mlp_w_down"][1]), 512),
        _pad_unit(_tm(f(inp["ple_w_proj"][1]), 512)), _tm(f(inp["ple_w_gate"][1]), 512),
    ]
    ws = np.ascontiguousarray(np.concatenate(parts, axis=1))
    assert ws.shape[1] == (L0_N + L1_N) * UW, ws.shape
    bc = lambda v: np.ascontiguousarray(np.broadcast_to(f(v)[None, :], (128, f(v).shape[0])))
    lnc = np.zeros((2, 128, LNC_W), np.float32)
    for l in range(2):
        for i, nm in enumerate(["ln1_g", "ln1_b", "ln2_g", "ln2_b"]):
            lnc[l, :, i * 1024:(i + 1) * 1024] = bc(inp[nm][l])
            lnc[l, :, LNC_C + i * 8:LNC_C + (i + 1) * 8] = f(inp[nm][l]).reshape(8, 128).T
        lnc[l, :, LNC_SB:LNC_W] = bc(inp["ple_b_gate"][l])
    lnc[1, :, 4 * 1024:5 * 1024] = bc(inp["swa_b_out"][0])
    wgl = np.ascontiguousarray(w_in[:, 3072:3088].reshape(8, 128, 16).transpose(1, 0, 2)).reshape(128, 128)
    wgu = np.ascontiguousarray(f(inp["gla_w_gk_up"][0]))
    bgk = np.ascontiguousarray(f(inp["gla_b_gk"][0]).reshape(4, 128).T)
    gnb = bc(inp["gla_norm_g"][0])
    bqkv = f(inp["swa_b_qkv"][0])
    bq = np.ascontiguousarray(bqkv[_qperm()].reshape(8, 128).T)
    bk = np.ascontiguousarray(bqkv[_kdup()].reshape(4, 128).T)
    bv = bc(bqkv[1280:1536])
    snk = bc(inp["swa_sinks"][0])
    small = np.ascontiguousarray(np.concatenate([bgk, bq, bk, snk, bv, gnb], axis=1))
    return dict(ws=ws, lnc=lnc, wgl=wgl, wgu=wgu, small=small)


SM_BGK, SM_BQ, SM_BK, SM_SNK, SM_BV, SM_GN, SM_W = 0, 4, 12, 16, 32, 288, 544
LNC_C = 5 * 1024
LNC_SB = LNC_C + 32
LNC_W = LNC_SB + 1024


def build_nc(n_groups=8, layers=(0, 1), dbg=False):
    T = n_groups * G
    nc = bass.Bass("TRN2", target_bir_lowering=False)
    x_d = nc.dram_tensor("x", [T, D], F32, kind="ExternalInput").ap()
    p_d = nc.dram_tensor("p", [2, T, 256], F32, kind="ExternalInput").ap()
    ws_d = nc.dram_tensor("ws", [128, (L0_N + L1_N) * UW], F32, kind="ExternalInput").ap()
    lnc_d = nc.dram_tensor("lnc", [2, 128, LNC_W], F32, kind="ExternalInput").ap()
    wgl_d = nc.dram_tensor("wgl", [128, 128], F32, kind="ExternalInput").ap()
    wgu_d = nc.dram_tensor("wgu", [16, 512], F32, kind="ExternalInput").ap()
    sm_d = nc.dram_tensor("small", [128, SM_W], F32, kind="ExternalInput").ap()
    y_d = nc.dram_tensor("y", [T, D], F32, kind="ExternalOutput").ap()
    wsb_d = nc.dram_tensor("wsb", [128, (L0_N + L1_N) * UW], BF16).ap()
    dbg_d = nc.dram_tensor("dbgo", [128, 8192], F32, kind="ExternalOutput").ap() if dbg else None

    S = Sched()
    st = ExitStack()
    with st:
        def sb(name, shape, dt):
            return st.enter_context(nc.sbuf_tensor(name, shape, dt))

        h_tok = sb("h_tok", [128, NT, D], F32)
        x_bf = sb("x_bf", [128, NT, D], BF16)
        xT = sb("xT", [128, 8, G], BF16)
        ogT = sb("ogT", [128, 8, G], BF16)
        ring = [sb(f"ring{i}", [128, UW], BF16) for i in range(NSLOT)]
        lnc_t = sb("lnc_sb", [128, LNC_SB], F32)
        lnc = lnc_t[:, 0:5 * D].rearrange("p (a d) -> p a d", a=5)
        small = sb("small_sb", [128, SM_W], F32)
        wgl = sb("wgl_sb", [128, 8, 16], BF16)
        wgu = sb("wgu_sb", [16, 512], BF16)
        ident = sb("ident", [128, 128], F32)
        identb = sb("identb", [128, 128], BF16)
        xnb = sb("xnb", [128, NT, D], BF16)
        nmr = sb("nmr", [128, NT], F32)
        ones_row = sb("ones_row", [1, 128], BF16)
        bg_row = sb("bg_row", [1, D], BF16)
        mask_le = sb("mask_le", [128, 128], BF16)
        mask_gt = sb("mask_gt", [128, 128], BF16)
        mask2 = sb("mask2", [128, 2, 128], BF16)
        rmask = sb("rmask", [128, G], BF16)
        nbgk = sb("nbgk", [128, 4], F32)
        esink = sb("esink", [128, 16], F32)
        S_f = sb("S_f", [128, 4, 256], F32)
        S_b = [sb(f"S_b{i}", [128, 4, 256], BF16) for i in range(2)]
        p_tok = sb("p_tok", [128, NT, 256], F32)
        pT = sb("pT", [128, 2, G], BF16)
        mv = sb("mv", [128, NT, 2], F32)
        bst = sb("bst", [128, NT, 12], F32)
        lnv = sb("lnv", [128, NT], F32)
        rstd = sb("rstd", [128, NT], F32)
        tmpa = [sb(f"tmpa{i}", [128, G], F32) for i in range(2)]
        tmpb = tmpa
        sgt = [sb(f"sgt{i}", [128, G], F32) for i in range(2)]
        bar = sb("bar", [128, 8], F32)
        wpj_all = sb("wpj_all", [128, 2, 2048], BF16)
        ARENA_F = 4400
        ARENA_B = 16896
        arF = sb("arF", [128, ARENA_F], F32)
        arB = sb("arB", [128, ARENA_B], BF16)
        hid = arB[:, 0:32 * G].rearrange("p (m t) -> p m t", m=32)
        o = 0
        csb = [arF[:, o + i * G:o + (i + 1) * G] for i in range(2)]; o += 2 * G
        Ep = [arF[:, o + i * G:o + (i + 1) * G] for i in range(2)]; o += 2 * G
        Em = [arF[:, o + i * G:o + (i + 1) * G] for i in range(2)]; o += 2 * G
        khT = [arF[:, o + i * 512:o + (i + 1) * 512].rearrange("p (h i) -> p h i", h=4) for i in range(2)]; o += 1024
        ebl = arF[:, o:o + 16].rearrange("p (h t) -> p h t", h=4); o += 16
        ssq = arF[:, o:o + 4]; o += 4
        rinv2 = [arF[:, o + 4 * i:o + 4 * i + 4] for i in range(2)]; o += 8
        lns = arF[:, o:o + 4]; o += 4
        junk = arF[:, o:o + 256]; o += 256
        assert o <= ARENA_F, o
        o = 0
        qtT = arB[:, o:o + 4 * G].rearrange("p (h t) -> p h t", h=4); o += 4 * G
        ktT = arB[:, o:o + 4 * G].rearrange("p (h t) -> p h t", h=4); o += 4 * G
        v_tok = arB[:, o:o + NT * D].rearrange("p (t d) -> p t d", t=NT); o += NT * D
        kh = [arB[:, o + i * 512:o + (i + 1) * 512].rearrange("p (h d) -> p h d", h=4) for i in range(2)]; o += 1024
        Abf = [arB[:, o + i * 512:o + (i + 1) * 512].rearrange("p (h i) -> p h i", h=4) for i in range(2)]; o += 1024
        gkl = arB[0:16, o:o + G]; o += G
        gate = arB[:, o:o + NT * D].rearrange("p (t d) -> p t d", t=NT); o += NT * D
        og = [arB[:, o + i * D:o + (i + 1) * D] for i in range(2)]; o += 2 * D
        assert o <= ARENA_B, o
        o = 0
        den = arF[:, o:o + 16]; o += 16
        rden = arF[:, o:o + 16]; o += 16
        o = 0
        qT = arB[:, o:o + 8 * G].rearrange("p (c t) -> p c t", c=8); o += 8 * G
        PT = [arB[:, o + i * 4096:o + (i + 1) * 4096].rearrange("p (k h i) -> p k h i", k=2, h=16) for i in range(2)]
        o += 2 * 4096
        Pex = [arB[:, o + i * 512:o + (i + 1) * 512] for i in range(4)]; o += 2048
        ao = [arB[:, o + i * D:o + (i + 1) * D] for i in range(2)]; o += 2 * D
        assert o <= ARENA_B, o
        kT = sb("kT", [128, 4, 128 * (NT + 1)], BF16)
        v_aug = sb("v_aug", [128, NT + 1, 4, 65], BF16)

        ps = [st.enter_context(nc.psum_tensor(f"ps{i}", [128, 512], F32)) for i in range(8)]
        psb = [p_[:].bitcast(BF16) for p_ in ps]
        bank_ctr = [0]

        reserved = set()

        def nb():
            while True:
                b = bank_ctr[0] % 8
                bank_ctr[0] += 1
                if b not in reserved:
                    return b

        def reserve(n):
            out = [nb() for _ in range(n)]
            reserved.update(out)
            return out

        def release(bs_):
            for b in bs_:
                reserved.discard(b)

        ecount = [0]

        def pick(*engs):
            ecount[0] += 1
            return engs[ecount[0] % len(engs)]

        arena_res = {"gla": set(), "swa": set(), "mlp": set()}
        cur_phase = [None]

        def R(name):
            arena_res[cur_phase[0]].add(name)
            return name

        def phase(newp):
            old = cur_phase[0]
            cur_phase[0] = newp
            if old is None or old == newp:
                return
            prev = sorted(arena_res[old])
            bb_ = nb()
            S.op("pe", (lambda bb_=bb_: lambda h: h.matmul(ps[bb_][:, 0:128], lhsT=ones_row[0:1, :], rhs=ones_row[0:1, :], start=True, stop=True))(),
                 reads=["ones_row"], writes=prev + ["arena_tok", f"ps{bb_}"])
            arena_res[old] = set()

        def AR(reads):
            return list(reads) + ["arena_tok"]

        seq = []
        for _g in range(n_groups):
            for _l in layers:
                if _l == 0:
                    seq += [(0, 2, UW), (0, 0, UW), (0, 1, UW)] + [(0, u, UW) for u in range(3, 24)] + [(0, 25, UW), (0, 26, UW)]
                else:
                    seq += [(1, 0, UW), (1, 1, UW), (1, 2, UW), (1, 3, 2048)] + [(1, u, UW) for u in range(4, 22)] + [(1, 23, UW), (1, 24, UW)]
        units_per_group = len(seq) // n_groups
        wctr = [0]
        issued = [0]
        PREF = NSLOT - 2

        def issue_next():
            j = issued[0]
            if j >= len(seq):
                return
            issued[0] += 1
            layer, uidx, ncols = seq[j]
            s_ = j % NSLOT
            base = (uidx if layer == 0 else L0_N + uidx) * UW
            ures = f"wsb{layer}_{uidx}"
            if j < units_per_group:
                S.op("pool", (lambda s_=s_, base=base, ncols=ncols: lambda h: h.dma_start(out=ring[s_][:, 0:ncols], in_=ws_d[:, base:base + ncols]))(),
                     writes=[f"ring{s_}"], dma_key=f"ringc{s_}")
                if n_groups > 1:
                    S.op("sp", (lambda s_=s_, base=base, ncols=ncols: lambda h: h.dma_start(out=wsb_d[:, base:base + ncols], in_=ring[s_][:, 0:ncols]))(),
                         reads=[f"ring{s_}"], writes=[ures], dma_key=f"wb{s_}")
            else:
                S.op("sp", (lambda s_=s_, base=base, ncols=ncols: lambda h: h.dma_start(out=ring[s_][:, 0:ncols], in_=wsb_d[:, base:base + ncols]))(),
                     reads=[ures], writes=[f"ring{s_}"], dma_key=f"ring{s_}")

        def wload(layer, uidx, ncols=UW):
            k = wctr[0]
            wctr[0] += 1
            assert seq[k] == (layer, uidx, ncols), (k, seq[k], layer, uidx, ncols)
            while issued[0] <= min(k + PREF, len(seq) - 1):
                issue_next()
            s_ = k % NSLOT
            return ring[s_], f"ring{s_}"

        S.op("sp", lambda h: h.dma_start(out=small[:], in_=sm_d[:, :]), writes=["small"], dma_key="c0")
        S.op("pool", lambda h: h.dma_start(out=wgl[:].rearrange("p k j -> p (k j)"), in_=wgl_d[:, :]), writes=["wgl"], dma_key="c1")
        S.op("pool", lambda h: h.dma_start(out=wgu[:], in_=wgu_d[:, :]), writes=["wgu"], dma_key="c2")
        for _l in range(2):
            _base = ((L0_UNITS["pj"]) if _l == 0 else (L0_N + L1_UNITS["pj"])) * UW
            S.op("pool", (lambda _l=_l, _base=_base: lambda h: h.dma_start(out=wpj_all[:, _l, :], in_=ws_d[:, _base:_base + 2048]))(), writes=["wpj"], dma_key=f"c3{_l}")
        S.op("pool", lambda h: h.memset(ident[:], 1.0), writes=["ident"])
        S.op("pool", lambda h: h.affine_select(out=ident[:], in_=ident[:], pattern=[[-1, 128]], compare_op=ALU.is_equal, fill=0.0, base=0, channel_multiplier=1), reads=["ident"], writes=["ident"])
        S.op("pool", lambda h: h.memset(ones_row[:], 1.0), writes=["ones_row"])
        S.op("pool", lambda h: h.memset(identb[:], 1.0), writes=["identb"])
        S.op("pool", lambda h: h.affine_select(out=identb[:], in_=identb[:], pattern=[[-1, 128]], compare_op=ALU.is_equal, fill=0.0, base=0, channel_multiplier=1), reads=["identb"], writes=["identb"])
        S.op("pool", lambda h: h.memset(mask_le[:], 1.0), writes=["mask_le"])
        S.op("pool", lambda h: h.affine_select(out=mask_le[:], in_=mask_le[:], pattern=[[1, 128]], compare_op=ALU.is_ge, fill=0.0, base=0, channel_multiplier=-1), reads=["mask_le"], writes=["mask_le"])
        S.op("pool", lambda h: h.memset(mask_gt[:], 1.0), writes=["mask_gt"])
        S.op("pool", lambda h: h.affine_select(out=mask_gt[:], in_=mask_gt[:], pattern=[[-1, 128]], compare_op=ALU.is_gt, fill=0.0, base=0, channel_multiplier=1), reads=["mask_gt"], writes=["mask_gt"])
        S.op("pool", lambda h: h.tensor_copy(out=mask2[:, 0, :], in_=mask_gt[:]), reads=["mask_gt"], writes=["mask2"])
        S.op("pool", lambda h: h.tensor_copy(out=mask2[:, 1, :], in_=mask_le[:]), reads=["mask_le", "mask2"], writes=["mask2"])
        S.op("pool", lambda h: h.memset(rmask[:], 1.0), writes=["rmask"])
        for t in range(NT):
            S.op("pool", (lambda t=t: lambda h: h.memset(rmask[:, t * 128:t * 128 + 1], 0.0))(), reads=["rmask"], writes=["rmask"])
        S.op("pool", lambda h: h.memset(S_f[:], 0.0), writes=[f"S_f{hd}" for hd in range(4)])
        S.op("pool", lambda h: h.memset(S_b[0][:], 0.0), writes=[f"S_b0_{hd}" for hd in range(4)])
        S.op("pool", lambda h: h.memset(S_b[1][:], 0.0), writes=[f"S_b1_{hd}" for hd in range(4)])
        S.op("pool", lambda h: h.memset(kT[:], 0.0), writes=["kT"])
        S.op("pool", lambda h: h.memset(v_aug[:], 0.0), writes=["v_aug"])
        S.op("pool", lambda h: h.memset(v_aug[:, :, :, 64:65], 1.0), reads=["v_aug"], writes=["v_aug"])
        S.op("dve", lambda h: h.tensor_scalar(out=nbgk[:], in0=small[:, SM_BGK:SM_BGK + 4], scalar1=-1.0, scalar2=None, op0=ALU.mult), reads=["small"], writes=["nbgk"])
        S.op("act", lambda h: h.activation(out=esink[:], in_=small[:, SM_SNK:SM_SNK + 16], func=AF.Exp), reads=["small"], writes=["esink"])

        def tile_to_T(src, src_res_prefix, t, dst, dst_res, src_is_bf16=False):
            if src_is_bf16:
                stg, stg_res = src, src_res_prefix
            else:
                stg, stg_res = xnb, "xnb"
                e = pick("act", "dve")
                def c_(h, t=t, e=e):
                    if e == "act":
                        return h.copy(out=xnb[:, t, :], in_=src[:, t, :])
                    return h.tensor_copy(out=xnb[:, t, :], in_=src[:, t, :])
                S.op(e, c_, reads=[f"{src_res_prefix}{t}"], writes=[f"xnb{t}"])
            for half in range(2):
                b = nb()
                def f(h, t=t, half=half, b=b, stg=stg):
                    ins = None
                    for j in range(4):
                        kc = half * 4 + j
                        ins = h.transpose(psb[b][:, j * 128:(j + 1) * 128], stg[:, t, kc * 128:(kc + 1) * 128], identb[:])
                    return ins
                S.op("pe", f, reads=[f"{stg_res}{t}", "identb"], writes=[f"ps{b}"])
                e = pick("act", "dve")
                def g(h, t=t, half=half, b=b, e=e):
                    o_ap = dst[:, half * 4:(half + 1) * 4, t * 128:(t + 1) * 128]
                    i_ap = psb[b][:, 0:512].rearrange("p (j i) -> p j i", j=4)
                    if e == "act":
                        return h.copy(out=o_ap, in_=i_ap)
                    return h.tensor_copy(out=o_ap, in_=i_ap)
                S.op(e, g, reads=[f"ps{b}"], writes=[f"{dst_res}{t}"])

        def transposes_to_xT(src, src_res_prefix, dst, dst_res, src_is_bf16=False):
            for t in range(NT):
                tile_to_T(src, src_res_prefix, t, dst, dst_res, src_is_bf16)

        def tm_proj(layer, ubase, src, src_res, evac, nunits_per_half=1, kchunks=8, halves=(0, 1)):
            for c in halves:
                banks = [nb() for _ in range(NT)]
                for uu in range(nunits_per_half):
                    wt, wres = wload(layer, ubase + c * nunits_per_half + uu)
                    for t in range(NT):
                        b = banks[t]
                        def f(h, t=t, b=b, wt=wt, uu=uu):
                            ins = None
                            for kc in range(8):
                                kk = uu * 8 + kc
                                ins = h.matmul(ps[b][:], lhsT=src[:, kk, t * 128:(t + 1) * 128], rhs=wt[:, kc * 512:(kc + 1) * 512],
                                               start=(kk == 0), stop=(kk == kchunks - 1))
                            return ins
                        S.op("pe", f, reads=[wres, f"{src_res}{t}"], writes=[f"ps{b}"])
                for t in range(NT):
                    evac(c, t, banks[t])

        def ln_stats(t):
            for hf in range(2):
                S.op("dve", (lambda t=t, hf=hf: lambda h: h.bn_stats(out=bst[:, t, hf * 6:(hf + 1) * 6], in_=h_tok[:, t, hf * 512:(hf + 1) * 512]))(),
                     reads=[f"h_tok{t}"], writes=[f"bst{t}"])
            S.op("dve", (lambda t=t: lambda h: h.bn_aggr(out=mv[:, t, :], in_=bst[:, t, :]))(), reads=[f"bst{t}"], writes=[f"mv{t}"])
            S.op("act", (lambda t=t: lambda h: h.activation(out=lnv[:, t:t + 1], in_=mv[:, t, 1:2], func=AF.Ln, bias=LN_EPS, scale=1.0))(), reads=[f"mv{t}"], writes=[f"lnv{t}"])
            S.op("act", (lambda t=t: lambda h: h.activation(out=rstd[:, t:t + 1], in_=lnv[:, t:t + 1], func=AF.Exp, scale=-0.5))(), reads=[f"lnv{t}"], writes=[f"rstd{t}"])

        def ln_norm(t):
            S.op("dve", (lambda t=t: lambda h: h.tensor_scalar(out=xnb[:, t, :], in0=h_tok[:, t, :], scalar1=mv[:, t, 0:1], scalar2=rstd[:, t:t + 1],
                                                               op0=ALU.subtract, op1=ALU.mult))(),
                 reads=[f"h_tok{t}", f"mv{t}", f"rstd{t}"], writes=[f"xnb{t}"])
            S.op("dve", (lambda t=t: lambda h: h.tensor_scalar(out=nmr[:, t:t + 1], in0=mv[:, t, 0:1], scalar1=rstd[:, t:t + 1], scalar2=-1.0, op0=ALU.mult, op1=ALU.mult))(),
                 reads=[f"mv{t}", f"rstd{t}"], writes=[f"nmr{t}"])

        def ln_residual(gi, bi):
            for t in range(NT):
                S.op("act", (lambda t=t: lambda h: h.activation(out=h_tok[:, t, :], in_=h_tok[:, t, :], func=AF.Identity, scale=rstd[:, t:t + 1], bias=nmr[:, t:t + 1]))(),
                     reads=[f"h_tok{t}", f"rstd{t}", f"nmr{t}"], writes=[f"h_tok{t}"])
                S.op("pool", (lambda t=t: lambda h: h.tensor_tensor(out=h_tok[:, t, :], in0=h_tok[:, t, :], in1=lnc[:, gi, :], op=ALU.mult))(),
                     reads=[f"h_tok{t}", "lnc"], writes=[f"h_tok{t}"])
                S.op("pool", (lambda t=t: lambda h: h.tensor_tensor(out=h_tok[:, t, :], in0=h_tok[:, t, :], in1=lnc[:, bi, :], op=ALU.add))(),
                     reads=[f"h_tok{t}", "lnc"], writes=[f"h_tok{t}"])

        def ln_hook(c, t):
            if c != 1:
                return
            ln_stats(t)
            if t >= 1:
                ln_norm(t - 1)
            if t == NT - 1:
                ln_norm(t)

        def ln_transpose(gi, bi, dst, dst_res):
            gc0 = LNC_C + gi * 8
            bc0 = LNC_C + bi * 8
            banks = reserve(8)
            for t in range(NT):
                def f(h, t=t, banks=banks):
                    ins = None
                    for kc in range(8):
                        ins = h.transpose(psb[banks[kc]][:, t * 128:(t + 1) * 128], xnb[:, t, kc * 128:(kc + 1) * 128], identb[:])
                    return ins
                S.op("pe", f, reads=[f"xnb{t}", "identb"], writes=[f"ps{b_}" for b_ in banks])
            for kc in range(8):
                e = "act" if kc % 2 == 0 else "dve"
                def g_(h, kc=kc, e=e, banks=banks):
                    o_ap = dst[:, kc, :]
                    i_ap = psb[banks[kc]][:, 0:G]
                    if e == "act":
                        return h.activation(out=o_ap, in_=i_ap, func=AF.Identity, scale=lnc_t[:, gc0 + kc:gc0 + kc + 1], bias=lnc_t[:, bc0 + kc:bc0 + kc + 1])
                    return h.tensor_scalar(out=o_ap, in0=i_ap, scalar1=lnc_t[:, gc0 + kc:gc0 + kc + 1], scalar2=lnc_t[:, bc0 + kc:bc0 + kc + 1], op0=ALU.mult, op1=ALU.add)
                S.op(e, g_, reads=[f"ps{banks[kc]}", "lnc"], writes=[f"{dst_res}{t}" for t in range(NT)])
            release(banks)

        def sigmoid_chain(src_ap, dst_ap, src_res, dst_res, tmp, tmp_res, negscale=True):
            S.op("act", lambda h: h.activation(out=tmp, in_=src_ap, func=AF.Exp, scale=-1.0), reads=src_res, writes=[tmp_res])
            S.op("act", lambda h: h.activation(out=tmp, in_=tmp, func=AF.Ln, bias=1.0, scale=1.0), reads=[tmp_res], writes=[tmp_res])
            S.op("act", lambda h: h.activation(out=dst_ap, in_=tmp, func=AF.Exp, scale=-1.0), reads=[tmp_res], writes=dst_res)

        def mlp_and_ple(layer, U, g, last):
            phase("mlp")
            ln_transpose(0, 1, xT, "xT")
            ln_residual(0, 1)
            for u in range(8):
                wt, wres = wload(layer, U["up"] + u)
                for m in range(4):
                    b = nb()
                    mi = u * 4 + m
                    def f(h, b=b, wt=wt, m=m):
                        ins = None
                        for kc in range(8):
                            ins = h.matmul(ps[b][:], lhsT=wt[:, (m * 8 + kc) * 128:(m * 8 + kc + 1) * 128], rhs=xT[:, kc, :], start=(kc == 0), stop=(kc == 7))
                        return ins
                    S.op("pe", f, reads=[wres] + [f"xT{t}" for t in range(NT)], writes=[f"ps{b}"])
                    i = mi % 2
                    S.op("act", (lambda b=b, i=i: lambda h: h.activation(out=tmpa[i][:], in_=ps[b][:], func=AF.Relu))(), reads=[f"ps{b}"], writes=[f"tmpa{i}"])
                    e = pick("dve", "pool")
                    S.op(e, (lambda i=i, mi=mi: lambda h: h.tensor_tensor(out=hid[:, mi, :], in0=tmpa[i][:], in1=tmpa[i][:], op=ALU.mult))(),
                         reads=AR([f"tmpa{i}"]), writes=[R(f"hid{mi}")])

            def evac_dn(c, t, b):
                S.op("dve", (lambda c=c, t=t, b=b: lambda h: h.scalar_tensor_tensor(out=h_tok[:, t, c * 512:(c + 1) * 512], in0=h_tok[:, t, c * 512:(c + 1) * 512],
                                                                                    scalar=ALPHA, in1=ps[b][:], op0=ALU.mult, op1=ALU.add))(),
                     reads=[f"ps{b}", f"h_tok{t}"], writes=[f"h_tok{t}"])
                ln_hook(c, t)
            tm_proj_hid(layer, U["dn"], evac_dn)
            ln_transpose(2, 3, xT, "xT")
            ln_residual(2, 3)
            if last and g + 1 < n_groups:
                transposes_to_xT(x_bf, "x_bf", ogT, "ogT", src_is_bf16=True)
            for t in range(NT):
                b = nb()
                def f(h, t=t, b=b):
                    ins = None
                    for k2 in range(2):
                        ins = h.transpose(ps[b][:, k2 * 128:(k2 + 1) * 128], p_tok[:, t, k2 * 128:(k2 + 1) * 128], ident[:])
                    return ins
                S.op("pe", f, reads=["p_tok", "ident"], writes=[f"ps{b}"])
                S.op("act", (lambda t=t, b=b: lambda h: h.copy(out=pT[:, :, t * 128:(t + 1) * 128], in_=ps[b][:, 0:256].rearrange("p (k i) -> p k i", k=2)))(),
                     reads=[f"ps{b}"], writes=[f"pT{t}"])
            wpj, wpj_res = wpj_all[:, layer, :], "wpj"

            def evac_gate(c, t, b):
                i = (c * NT + t) % 2
                S.op("act", (lambda b=b, i=i: lambda h: h.activation(out=sgt[i][:], in_=ps[b][:], func=AF.Tanh, scale=0.5))(), reads=[f"ps{b}"], writes=[f"sgt{i}"])
                b2 = nb()
                def f(h, c=c, t=t, b2=b2):
                    ins = None
                    for k2 in range(2):
                        ins = h.matmul(ps[b2][:], lhsT=pT[:, k2, t * 128:(t + 1) * 128], rhs=wpj[:, (c * 2 + k2) * 512:(c * 2 + k2 + 1) * 512], start=(k2 == 0), stop=(k2 == 1))
                    return ins
                S.op("pe", f, reads=[wpj_res, f"pT{t}"], writes=[f"ps{b2}"])
                S.op("dve", (lambda i=i, b2=b2: lambda h: h.scalar_tensor_tensor(out=sgt[i][:], in0=sgt[i][:], scalar=1.0, in1=ps[b2][:], op0=ALU.add, op1=ALU.mult))(),
                     reads=[f"sgt{i}", f"ps{b2}"], writes=[f"sgt{i}"])
                S.op("dve", (lambda c=c, t=t, i=i: lambda h: h.scalar_tensor_tensor(out=h_tok[:, t, c * 512:(c + 1) * 512], in0=sgt[i][:], scalar=0.5, in1=h_tok[:, t, c * 512:(c + 1) * 512], op0=ALU.mult, op1=ALU.add))(),
                     reads=[f"sgt{i}", f"h_tok{t}"], writes=[f"h_tok{t}"])
            wts = [wload(layer, U["gt"]), wload(layer, U["gt"] + 1)]
            for t in range(NT):
                for c in range(2):
                    wt, wres = wts[c]
                    b = nb()
                    def f(h, t=t, b=b, wt=wt, c=c):
                        for kc in range(8):
                            h.matmul(ps[b][:], lhsT=xT[:, kc, t * 128:(t + 1) * 128], rhs=wt[:, kc * 512:(kc + 1) * 512], start=(kc == 0), stop=False)
                        return h.matmul(ps[b][:], lhsT=ones_row[0:1, :], rhs=bg_row[0:1, c * 512:(c + 1) * 512], start=False, stop=True)
                    S.op("pe", f, reads=[wres, f"xT{t}", "ones_row", "bg_row"], writes=[f"ps{b}"])
                    evac_gate(c, t, b)
            if not last:
                for t in range(NT):
                    tile_to_T(h_tok, "h_tok", t, xT, "xT")

        def tm_proj_hid(layer, ubase, evac):
            for c in range(2):
                banks = [nb() for _ in range(NT)]
                for uu in range(4):
                    wt, wres = wload(layer, ubase + c * 4 + uu)
                    for t in range(NT):
                        b = banks[t]
                        def f(h, t=t, b=b, wt=wt, uu=uu):
                            ins = None
                            for kc in range(8):
                                kk = uu * 8 + kc
                                ins = h.matmul(ps[b][:], lhsT=hid[:, kk, t * 128:(t + 1) * 128], rhs=wt[:, kc * 512:(kc + 1) * 512], start=(kk == 0), stop=(kk == 31))
                            return ins
                        S.op("pe", f, reads=AR([wres] + [f"hid{uu * 8 + kc}" for kc in range(8)]), writes=[f"ps{b}"])
                for t in range(NT):
                    evac(c, t, banks[t])

        for g in range(n_groups):
            if g == 0:
                S.op("pool", (lambda g=g: lambda h: h.dma_start(out=x_bf[:], in_=x_d[g * G:(g + 1) * G, :].rearrange("(t p) d -> p t d", p=128)))(),
                     writes=[f"x_bf{t}" for t in range(NT)], dma_key="xbf")
            S.op("pool", (lambda g=g: lambda h: h.dma_start(out=h_tok[:], in_=x_d[g * G:(g + 1) * G, :].rearrange("(t p) d -> p t d", p=128)))(),
                 writes=[f"h_tok{t}" for t in range(NT)], dma_key="xin")
            for layer in layers:
                S.op("pool", (lambda layer=layer: lambda h: h.dma_start(out=lnc_t[:], in_=lnc_d[layer, :, 0:LNC_SB]))(), writes=["lnc"], dma_key="lnc")
                S.op("pool", (lambda layer=layer, g=g: lambda h: h.dma_start(out=p_tok[:], in_=p_d[layer, g * G:(g + 1) * G, :].rearrange("(t p) d -> p t d", p=128)))(),
                     writes=["p_tok"], dma_key="ptok")
                S.op("pool", (lambda layer=layer: lambda h: h.dma_start(out=bg_row[:], in_=lnc_d[layer, 0:1, LNC_SB:LNC_W]))(), writes=["bg_row"], dma_key="bgrow")
                first = (layer == layers[0])
                last = (layer == layers[-1])
                if first and g == 0:
                    transposes_to_xT(x_bf, "x_bf", ogT, "ogT", src_is_bf16=True)
                inT, inres = (ogT, "ogT") if first else (xT, "xT")
                xres = [f"{inres}{t}" for t in range(NT)]
                def evac_wo(c, t, b, layer=layer, first=first):
                    rsrc = h_tok
                    rres = f"h_tok{t}"
                    S.op("dve", (lambda c=c, t=t, b=b, rsrc=rsrc: lambda h: h.scalar_tensor_tensor(out=h_tok[:, t, c * 512:(c + 1) * 512], in0=rsrc[:, t, c * 512:(c + 1) * 512],
                                                                                        scalar=ALPHA, in1=ps[b][:], op0=ALU.mult, op1=ALU.add))(),
                         reads=[f"ps{b}", rres, f"h_tok{t}"], writes=[f"h_tok{t}"])
                    if layer == 1:
                        S.op("dve", (lambda c=c, t=t: lambda h: h.tensor_tensor(out=h_tok[:, t, c * 512:(c + 1) * 512], in0=h_tok[:, t, c * 512:(c + 1) * 512], in1=lnc[:, 4, c * 512:(c + 1) * 512], op=ALU.add))(),
                             reads=[f"h_tok{t}", "lnc"], writes=[f"h_tok{t}"])
                    ln_hook(c, t)
                wo_units = []

                def wo_load(U):
                    wo_units.append(wload(layer, U["wo"]))
                    wo_units.append(wload(layer, U["wo"] + 1))

                def WO(t):
                    for c in range(2):
                        wt, wres = wo_units[c]
                        b = nb()
                        def f(h, t=t, b=b, wt=wt):
                            ins = None
                            for kc in range(8):
                                ins = h.matmul(ps[b][:], lhsT=ogT[:, kc, t * 128:(t + 1) * 128], rhs=wt[:, kc * 512:(kc + 1) * 512], start=(kc == 0), stop=(kc == 7))
                            return ins
                        S.op("pe", f, reads=[wres, f"ogT{t}"], writes=[f"ps{b}"])
                        evac_wo(c, t, b)

                if layer == 0:
                    U = L0_UNITS
                    phase("gla")
                    b = nb()
                    def f(h, b=b, inT=inT):
                        ins = None
                        for kc in range(8):
                            ins = h.matmul(ps[b][0:16, :], lhsT=wgl[:, kc, :], rhs=inT[:, kc, :], start=(kc == 0), stop=(kc == 7))
                        return ins
                    S.op("pe", f, reads=["wgl"] + xres, writes=[f"ps{b}"])
                    S.op("dve", (lambda b=b: lambda h: h.tensor_copy(out=gkl, in_=ps[b][0:16, :]))(), reads=AR([f"ps{b}"]), writes=[R("gkl")])
                    def evac_v(c, t, b):
                        e = pick("act", "dve")
                        def gfn(h, c=c, t=t, b=b, e=e):
                            if e == "act":
                                return h.copy(out=v_tok[:, t, c * 512:(c + 1) * 512], in_=ps[b][:])
                            return h.tensor_copy(out=v_tok[:, t, c * 512:(c + 1) * 512], in_=ps[b][:])
                        S.op(e, gfn, reads=AR([f"ps{b}"]), writes=[R(f"v_tok{t}_{c}")])
                    tm_proj(0, U["v"], inT, inres, evac_v, halves=(0,))
                    wq_t, wq_res = wload(0, U["q"])
                    wk_t, wk_res = wload(0, U["k"])
                    bgs = reserve(4)
                    for hd in range(4):
                        bg = bgs[hd]
                        S.op("pe", (lambda bg=bg, hd=hd: lambda h: h.matmul(ps[bg][:], lhsT=wgu[:, hd * 128:(hd + 1) * 128], rhs=gkl, start=True, stop=True))(),
                             reads=AR(["wgu", "gkl"]), writes=[f"ps{bg}"])

                    def chain(hd):
                        bg = bgs[hd]
                        i = hd % 2
                        S.op("act", (lambda bg=bg, hd=hd, i=i: lambda h: h.activation(out=tmpa[i][:], in_=ps[bg][:], func=AF.Exp, scale=-1.0, bias=nbgk[:, hd:hd + 1]))(),
                             reads=[f"ps{bg}", "nbgk"], writes=[f"tmpa{i}"])
                        release([bg])
                        S.op("act", (lambda i=i: lambda h: h.activation(out=tmpa[i][:], in_=tmpa[i][:], func=AF.Ln, bias=1.0, scale=1.0))(), reads=[f"tmpa{i}"], writes=[f"tmpa{i}"])
                        S.op("dve", (lambda i=i: lambda h: h.tensor_tensor_scan(out=csb[i], data0=rmask[:], data1=tmpa[i][:], initial=0.0, op0=ALU.mult, op1=ALU.add))(),
                             reads=AR([f"tmpa{i}", "rmask"]), writes=[R(f"csb{i}")])
                        S.op("act", (lambda i=i: lambda h: h.activation(out=Ep[i], in_=csb[i], func=AF.Exp, scale=-1.0 / GLA_TAU))(), reads=AR([f"csb{i}"]), writes=[R(f"Ep{i}")])
                        S.op("act", (lambda i=i: lambda h: h.activation(out=Em[i], in_=csb[i], func=AF.Exp, scale=1.0 / GLA_TAU))(), reads=AR([f"csb{i}"]), writes=[R(f"Em{i}")])
                        S.op("pool", (lambda hd=hd, i=i: lambda h: h.tensor_copy(out=ebl[:, hd, :], in_=Ep[i].rearrange("p (t i) -> p t i", t=NT)[:, :, 127]))(),
                             reads=AR([f"Ep{i}"]), writes=[R(f"ebl{hd}")])

                    def qk(hd):
                        i = hd % 2
                        bq_, bk_ = nb(), nb()
                        for (bb, wt, wres) in ((bq_, wq_t, wq_res), (bk_, wk_t, wk_res)):
                            def f(h, bb=bb, wt=wt, hd=hd, inT=inT):
                                ins = None
                                for kc in range(8):
                                    ins = h.matmul(ps[bb][:], lhsT=wt[:, (hd * 8 + kc) * 128:(hd * 8 + kc + 1) * 128], rhs=inT[:, kc, :], start=(kc == 0), stop=(kc == 7))
                                return ins
                            S.op("pe", f, reads=[wres] + xres, writes=[f"ps{bb}"])
                        S.op("dve", (lambda hd=hd, bq_=bq_, i=i: lambda h: h.scalar_tensor_tensor(out=qtT[:, hd, :], in0=ps[bq_][:], scalar=float(128 ** -0.5), in1=Ep[i], op0=ALU.mult, op1=ALU.mult))(),
                             reads=AR([f"ps{bq_}", f"Ep{i}"]), writes=[R(f"qtT{hd}")])
                        S.op("dve", (lambda hd=hd, bk_=bk_, i=i: lambda h: h.tensor_tensor(out=ktT[:, hd, :], in0=ps[bk_][:], in1=Em[i], op=ALU.mult))(),
                             reads=AR([f"ps{bk_}", f"Em{i}"]), writes=[R(f"ktT{hd}")])

                    chain(0); chain(1); qk(0); qk(1); chain(2); qk(2); chain(3); qk(3)
                    tm_proj(0, U["v"], inT, inres, evac_v, halves=(1,))
                    def evac_r(c, t, b):
                        i = (c * NT + t) % 2
                        S.op("act", (lambda b=b, i=i: lambda h: h.activation(out=sgt[i][:], in_=ps[b][:], func=AF.Tanh, scale=0.5))(), reads=[f"ps{b}"], writes=[f"sgt{i}"])
                        S.op("dve", (lambda c=c, t=t, b=b, i=i: lambda h: h.scalar_tensor_tensor(out=sgt[i][:], in0=sgt[i][:], scalar=1.0, in1=ps[b][:], op0=ALU.add, op1=ALU.mult))(),
                             reads=[f"sgt{i}", f"ps{b}"], writes=[f"sgt{i}"])
                        S.op("pool", (lambda c=c, t=t, i=i: lambda h: h.tensor_tensor(out=gate[:, t, c * 512:(c + 1) * 512].rearrange("p (a e) -> p a e", a=2), in0=sgt[i][:].rearrange("p (a e) -> p a e", a=2),
                                                                                 in1=small[:, SM_GN:SM_GN + 256].unsqueeze(1).broadcast_to([128, 2, 256]), op=ALU.mult))(),
                             reads=AR([f"sgt{i}", "small"]), writes=[R(f"gate{t}_{c}")])
                    tm_proj(0, U["r"], inT, inres, evac_r)

                    def stage_AB(t):
                        ts = slice(t * 128, (t + 1) * 128)
                        pi = t % 2
                        ba = nb()
                        def f(h, ba=ba, ts=ts):
                            ins = None
                            for hd in range(4):
                                ins = h.matmul(ps[ba][:, hd * 128:(hd + 1) * 128], lhsT=ktT[:, hd, ts], rhs=qtT[:, hd, ts], start=True, stop=True)
                            return ins
                        S.op("pe", f, reads=AR([f"ktT{hd}" for hd in range(4)] + [f"qtT{hd}" for hd in range(4)]), writes=[f"ps{ba}"])
                        S.op("dve", (lambda ba=ba, pi=pi: lambda h: h.tensor_tensor(out=Abf[pi], in0=ps[ba][:].rearrange("p (h i) -> p h i", h=4),
                                                                                 in1=mask_le[:].unsqueeze(1).broadcast_to([128, 4, 128]), op=ALU.mult))(),
                             reads=AR([f"ps{ba}", "mask_le"]), writes=[R(f"Abf{pi}")])
                        for hd in range(4):
                            S.op("act", (lambda pi=pi, ts=ts, t=t, hd=hd: lambda h: h.activation(out=khT[pi][:, hd, :], in_=ktT[:, hd, ts], func=AF.Copy, scale=ebl[:, hd, t:t + 1]))(),
                                 reads=AR([f"ktT{hd}", f"ebl{hd}"]), writes=[R(f"khT{pi}_{hd}")])
                        bt = nb()
                        def f2(h, bt=bt, pi=pi):
                            ins = None
                            for hd in range(4):
                                ins = h.transpose(ps[bt][:, hd * 128:(hd + 1) * 128], khT[pi][:, hd, :], ident[:])
                            return ins
                        S.op("pe", f2, reads=AR([f"khT{pi}_{hd}" for hd in range(4)] + ["ident"]), writes=[f"ps{bt}"])
                        S.op("act", (lambda bt=bt, pi=pi: lambda h: h.copy(out=kh[pi], in_=ps[bt][:].rearrange("p (h d) -> p h d", h=4)))(), reads=AR([f"ps{bt}"]), writes=[R(f"kh{pi}")])

                    pend = {}

                    def stage_D(t):
                        tg = g * NT + t
                        pi = t % 2
                        sb_w = S_b[(tg + 1) % 2]
                        sbw_res = f"S_b{(tg + 1) % 2}"
                        for half in range(2):
                            bs = nb()
                            def f(h, half=half, bs=bs, t=t, pi=pi):
                                ins = None
                                for q_ in range(2):
                                    hd = half * 2 + q_
                                    ins = h.matmul(ps[bs][:, q_ * 256:(q_ + 1) * 256], lhsT=kh[pi][:, hd, :], rhs=v_tok[:, t, hd * 256:(hd + 1) * 256], start=True, stop=True)
                                return ins
                            S.op("pe", f, reads=AR([f"kh{pi}", f"v_tok{t}_{half}"]), writes=[f"ps{bs}"])
                            for q_ in range(2):
                                hd = half * 2 + q_
                                S.op("dve", (lambda bs=bs, hd=hd, t=t, q_=q_, sb_w=sb_w: lambda h: h.scalar_tensor_tensor(out=sb_w[:, hd, :], in0=S_f[:, hd, :], scalar=ebl[:, hd, t:t + 1], in1=ps[bs][:, q_ * 256:(q_ + 1) * 256], op0=ALU.mult, op1=ALU.add))(),
                                     reads=AR([f"ps{bs}", f"ebl{hd}", f"S_f{hd}"]), writes=[f"{sbw_res}_{hd}"])
                            for q_ in range(2):
                                hd = half * 2 + q_
                                S.op("dve", (lambda bs=bs, hd=hd, t=t, q_=q_: lambda h: h.scalar_tensor_tensor(out=S_f[:, hd, :], in0=S_f[:, hd, :], scalar=ebl[:, hd, t:t + 1], in1=ps[bs][:, q_ * 256:(q_ + 1) * 256], op0=ALU.mult, op1=ALU.add))(),
                                     reads=AR([f"ps{bs}", f"ebl{hd}", f"S_f{hd}"]), writes=[f"S_f{hd}"])

                    def stage_C(t):
                        tg = g * NT + t
                        ts = slice(t * 128, (t + 1) * 128)
                        pi = t % 2
                        sb_r = S_b[tg % 2]
                        sbr_res = f"S_b{tg % 2}"
                        bo = reserve(2)
                        for half in range(2):
                            def f(h, half=half, bo=bo, ts=ts, t=t, sb_r=sb_r, pi=pi):
                                ins = None
                                for q_ in range(2):
                                    hd = half * 2 + q_
                                    cs = slice(q_ * 256, q_ * 256 + 256)
                                    h.matmul(ps[bo[half]][:, cs], lhsT=Abf[pi][:, hd, :], rhs=v_tok[:, t, hd * 256:(hd + 1) * 256], start=True, stop=False)
                                    ins = h.matmul(ps[bo[half]][:, cs], lhsT=qtT[:, hd, ts], rhs=sb_r[:, hd, :], start=False, stop=True)
                                return ins
                            S.op("pe", f, reads=AR([f"qtT{half * 2}", f"qtT{half * 2 + 1}", f"{sbr_res}_{half * 2}", f"{sbr_res}_{half * 2 + 1}", f"Abf{pi}", f"v_tok{t}_{half}"]), writes=[f"ps{bo[half]}"])
                        oi = t % 2
                        for hd in range(4):
                            bb = bo[hd // 2]
                            cs = slice((hd % 2) * 256, (hd % 2) * 256 + 256)
                            S.op("act", (lambda bb=bb, cs=cs, hd=hd: lambda h: h.activation(out=junk, in_=ps[bb][:, cs], func=AF.Square, accum_out=ssq[:, hd:hd + 1]))(),
                                 reads=AR([f"ps{bb}"]), writes=[R("junk"), R(f"ssq{hd}")])
                        S.op("act", lambda h: h.activation(out=lns, in_=ssq, func=AF.Ln, bias=RMS_EPS, scale=1.0 / 256.0), reads=AR([f"ssq{hd}" for hd in range(4)]), writes=[R("lns")])
                        S.op("act", (lambda t=t: lambda h: h.activation(out=rinv2[t % 2], in_=lns, func=AF.Exp, scale=-0.5, bias=HALF_LN))(), reads=AR(["lns"]), writes=[R(f"rinv{t % 2}")])
                        pend_og[t] = bo

                    def stage_Cog(t):
                        bo = pend_og.pop(t)
                        oi = t % 2
                        ri = t % 2
                        for hd in range(4):
                            bb = bo[hd // 2]
                            cs = slice((hd % 2) * 256, (hd % 2) * 256 + 256)
                            S.op("dve", (lambda bb=bb, cs=cs, hd=hd, t=t, oi=oi, ri=ri: lambda h: h.scalar_tensor_tensor(out=og[oi][:, hd * 256:(hd + 1) * 256], in0=ps[bb][:, cs], scalar=rinv2[ri][:, hd:hd + 1],
                                                                                                              in1=gate[:, t, hd * 256:(hd + 1) * 256], op0=ALU.mult, op1=ALU.mult))(),
                                 reads=AR([f"ps{bb}", f"rinv{ri}", f"gate{t}_{hd // 2}"]), writes=[R(f"og{oi}_{hd}")])
                        release(bo)

                    def stage_E(t):
                        ts = slice(t * 128, (t + 1) * 128)
                        oi = t % 2
                        for half in range(2):
                            b = nb()
                            def f(h, half=half, b=b, oi=oi):
                                ins = None
                                for j in range(4):
                                    kc = half * 4 + j
                                    ins = h.transpose(psb[b][:, j * 128:(j + 1) * 128], og[oi][:, kc * 128:(kc + 1) * 128], identb[:])
                                return ins
                            S.op("pe", f, reads=AR([f"og{oi}_{hd}" for hd in range(4)] + ["identb"]), writes=[f"ps{b}"])
                            e = pick("act", "dve")
                            def ev(h, half=half, b=b, ts=ts, e=e):
                                o_ap = ogT[:, half * 4:(half + 1) * 4, ts]
                                i_ap = psb[b][:, 0:512].rearrange("p (j i) -> p j i", j=4)
                                if e == "act":
                                    return h.copy(out=o_ap, in_=i_ap)
                                return h.tensor_copy(out=o_ap, in_=i_ap)
                            S.op(e, ev, reads=[f"ps{b}"], writes=[f"ogT{t}"])

                    wo_load(U)
                    pend_og = {}
                    stage_AB(0)
                    stage_AB(1)
                    for t in range(NT):
                        stage_D(t)
                        stage_C(t)
                        if t + 2 < NT:
                            stage_AB(t + 2)
                        if t >= 1:
                            stage_Cog(t - 1)
                            stage_E(t - 1)
                    stage_Cog(NT - 1)
                    stage_E(NT - 1)
                    for t in range(NT):
                        WO(t)
                    if dbg and g == 0:
                        dumps = [(csb[1], 0, 512), (Ep[1], 512, 512), (qtT[:, 3, :], 1024, 512), (ktT[:, 3, :], 1536, 512), (v_tok[:, 0, :], 2048, 1024),
                                 (gate[:, 0, :], 3072, 1024), (og[1], 4096, 1024), (ogT[:, 0:2, :], 5120, 1024), (S_f[:], 6144, 1024), (og[0], 7168, 1024)]
                        allres = sorted(arena_res["gla"]) + [f"ogT{t}" for t in range(NT)] + [f"S_f{hd}" for hd in range(4)]
                        for di, (ap_, off, n) in enumerate(dumps):
                            o_ap = dbg_d[:, off:off + n]
                            if len(ap_.shape) == 3:
                                o_ap = o_ap.rearrange("p (a b) -> p a b", a=ap_.shape[1])
                            S.op("pool", (lambda ap_=ap_, o_ap=o_ap: lambda h: h.dma_start(out=o_ap, in_=ap_))(), reads=AR(allres), writes=["dbg_out"], dma_key=f"dbg{di}")
                else:
                    U = L1_UNITS
                    phase("swa")
                    for uq in range(2):
                        wt, wres = wload(1, U["q"] + uq)
                        for m in range(4):
                            c = uq * 4 + m
                            b = nb()
                            def f(h, b=b, wt=wt, m=m, inT=inT):
                                ins = None
                                for kc in range(8):
                                    ins = h.matmul(ps[b][:], lhsT=wt[:, (m * 8 + kc) * 128:(m * 8 + kc + 1) * 128], rhs=inT[:, kc, :], start=(kc == 0), stop=(kc == 7))
                                return ins
                            S.op("pe", f, reads=[wres] + xres, writes=[f"ps{b}"])
                            S.op("dve", (lambda b=b, c=c: lambda h: h.tensor_scalar(out=qT[:, c, :], in0=ps[b][:], scalar1=small[:, SM_BQ + c:SM_BQ + c + 1], scalar2=0.125, op0=ALU.add, op1=ALU.mult))(),
                                 reads=AR([f"ps{b}", "small"]), writes=[R(f"qT{c}")])
                    wt, wres = wload(1, U["k"])
                    for kv in range(4):
                        b = nb()
                        def f(h, b=b, wt=wt, kv=kv, inT=inT):
                            ins = None
                            for kc in range(8):
                                ins = h.matmul(ps[b][:], lhsT=wt[:, (kv * 8 + kc) * 128:(kv * 8 + kc + 1) * 128], rhs=inT[:, kc, :], start=(kc == 0), stop=(kc == 7))
                            return ins
                        S.op("pe", f, reads=[wres] + xres, writes=[f"ps{b}"])
                        S.op("act", (lambda b=b, kv=kv: lambda h: h.activation(out=kT[:, kv, 128:128 + G], in_=ps[b][:], func=AF.Identity, bias=small[:, SM_BK + kv:SM_BK + kv + 1], scale=1.0))(),
                             reads=[f"ps{b}", "small"], writes=[f"kT{kv}"])
                    wt, wres = wload(1, U["v"], ncols=2048)
                    for t in range(NT):
                        b = nb()
                        def f(h, b=b, wt=wt, t=t, inT=inT):
                            ins = None
                            for kc in range(8):
                                ins = h.matmul(ps[b][:, 0:256], lhsT=inT[:, kc, t * 128:(t + 1) * 128], rhs=wt[:, kc * 256:(kc + 1) * 256], start=(kc == 0), stop=(kc == 7))
                            return ins
                        S.op("pe", f, reads=[wres, f"{inres}{t}"], writes=[f"ps{b}"])
                        S.op("dve", (lambda b=b, t=t: lambda h: h.tensor_tensor(out=v_aug[:, t + 1, :, 0:64], in0=ps[b][:, 0:256].rearrange("p (k d) -> p k d", k=4),
                                                                             in1=small[:, SM_BV:SM_BV + 256].rearrange("p (k d) -> p k d", k=4), op=ALU.add))(),
                             reads=[f"ps{b}", "small"], writes=[f"v_aug{t + 1}"])
                    def rec_S(t, kv):
                        pi = t % 2
                        PTt = PT[pi]
                        blo, bhi = nb(), nb()
                        def f(h, blo=blo, bhi=bhi, kv=kv, t=t):
                            ins = None
                            for kt in range(2):
                                kcols = slice((t + kt) * 128, (t + kt + 1) * 128)
                                h.matmul(ps[blo][:, kt * 256:(kt + 1) * 256], lhsT=kT[0:64, kv, kcols], rhs=qT[0:64, 2 * kv:2 * kv + 2, t * 128:(t + 1) * 128], start=True, stop=True)
                                ins = h.matmul(ps[bhi][:, kt * 256:(kt + 1) * 256], lhsT=kT[64:128, kv, kcols], rhs=qT[64:128, 2 * kv:2 * kv + 2, t * 128:(t + 1) * 128], start=True, stop=True)
                            return ins
                        S.op("pe", f, reads=AR([f"kT{kv}", f"qT{2 * kv}", f"qT{2 * kv + 1}"]), writes=[f"ps{blo}", f"ps{bhi}"])
                        for hi, bb in enumerate((blo, bhi)):
                            xi = (kv * 2 + hi) % 4
                            S.op("act", (lambda bb=bb, xi=xi: lambda h: h.activation(out=Pex[xi], in_=ps[bb][:], func=AF.Exp))(), reads=AR([f"ps{bb}"]), writes=[R(f"Pex{xi}")])
                            e = pick("dve", "pool")
                            S.op(e, (lambda xi=xi, kv=kv, hi=hi, PTt=PTt: lambda h: h.tensor_tensor(
                                out=PTt[:, :, kv * 4 + hi * 2:kv * 4 + hi * 2 + 2, :], in0=Pex[xi].rearrange("p (k g i) -> p k g i", k=2, g=2),
                                in1=mask2[:].unsqueeze(2).broadcast_to([128, 2, 2, 128]), op=ALU.mult))(),
                                 reads=AR([f"Pex{xi}", "mask2"]), writes=[R(f"PT{pi}_{kv}_{hi}")])

                    def rec_PV(t, kv):
                        pi = t % 2
                        oi = t % 2
                        PTt = PT[pi]
                        b = nb()
                        def f(h, b=b, kv=kv, t=t, PTt=PTt):
                            ins = None
                            for gq in range(4):
                                hdx = kv * 4 + gq
                                h.matmul(ps[b][:, gq * 65:(gq + 1) * 65], lhsT=PTt[:, 0, hdx, :], rhs=v_aug[:, t, kv, :], start=True, stop=False)
                                ins = h.matmul(ps[b][:, gq * 65:(gq + 1) * 65], lhsT=PTt[:, 1, hdx, :], rhs=v_aug[:, t + 1, kv, :], start=False, stop=True)
                            return ins
                        S.op("pe", f, reads=AR([f"PT{pi}_{kv}_0", f"PT{pi}_{kv}_1", f"v_aug{t}", f"v_aug{t + 1}"]), writes=[f"ps{b}"])
                        pv = ps[b][:, 0:260].rearrange("p (g d) -> p g d", g=4)
                        S.op("dve", (lambda b=b, kv=kv, pv=pv: lambda h: h.tensor_tensor(out=den[:, kv * 4:(kv + 1) * 4], in0=pv[:, :, 64], in1=esink[:, kv * 4:(kv + 1) * 4], op=ALU.add))(),
                             reads=AR([f"ps{b}", "esink"]), writes=[R(f"den{kv}")])
                        S.op("dve", (lambda kv=kv: lambda h: h.reciprocal(out=rden[:, kv * 4:(kv + 1) * 4], in_=den[:, kv * 4:(kv + 1) * 4]))(), reads=AR([f"den{kv}"]), writes=[R(f"rden{kv}")])
                        S.op("dve", (lambda b=b, kv=kv, oi=oi, pv=pv: lambda h: h.tensor_tensor(out=ao[oi][:, kv * 256:(kv + 1) * 256].rearrange("p (g d) -> p g d", g=4), in0=pv[:, :, 0:64],
                                                                                          in1=rden[:, kv * 4:(kv + 1) * 4].unsqueeze(2).broadcast_to([128, 4, 64]), op=ALU.mult))(),
                             reads=AR([f"ps{b}", f"rden{kv}"]), writes=[R(f"ao{oi}_{kv * 4 + gq}") for gq in range(4)])

                    def rec_T(t):
                        oi = t % 2
                        ts = slice(t * 128, (t + 1) * 128)
                        for half in range(2):
                            b = nb()
                            def f(h, half=half, b=b, oi=oi):
                                ins = None
                                for j in range(4):
                                    kc = half * 4 + j
                                    ins = h.transpose(psb[b][:, j * 128:(j + 1) * 128], ao[oi][:, kc * 128:(kc + 1) * 128], identb[:])
                                return ins
                            S.op("pe", f, reads=AR([f"ao{oi}_{hdx}" for hdx in range(16)] + ["identb"]), writes=[f"ps{b}"])
                            S.op("dve", (lambda half=half, b=b, ts=ts: lambda h: h.tensor_copy(out=ogT[:, half * 4:(half + 1) * 4, ts], in_=psb[b][:, 0:512].rearrange("p (j i) -> p j i", j=4)))(),
                                 reads=[f"ps{b}"], writes=[f"ogT{t}"])

                    wo_load(U)
                    units = [(t, kv) for t in range(NT) for kv in range(4)]
                    LAG, TL = 3, 2
                    for i in range(len(units) + LAG + TL):
                        if i < len(units):
                            rec_S(*units[i])
                        j = i - LAG
                        if 0 <= j < len(units):
                            rec_PV(*units[j])
                        k = i - LAG - TL
                        if 0 <= k < len(units) and units[k][1] == 3:
                            rec_T(units[k][0])
                    for t in range(NT):
                        WO(t)
                    S.op("pool", lambda h: h.tensor_copy(out=kT[:, :, 0:128], in_=kT[:, :, G:G + 128]), reads=[f"kT{kv}" for kv in range(4)], writes=[f"kT{kv}" for kv in range(4)])
                    S.op("pool", lambda h: h.tensor_copy(out=v_aug[:, 0, :, :], in_=v_aug[:, NT, :, :]), reads=[f"v_aug{NT}", "v_aug0"], writes=["v_aug0"])

                if first and g + 1 < n_groups:
                    S.op("pool", (lambda g=g: lambda h: h.dma_start(out=x_bf[:], in_=x_d[(g + 1) * G:(g + 2) * G, :].rearrange("(t p) d -> p t d", p=128)))(),
                         writes=[f"x_bf{t}" for t in range(NT)], dma_key="xbf")
                mlp_and_ple(layer, U, g, last)
            S.op("pool", (lambda g=g: lambda h: h.dma_start(out=y_d[g * G:(g + 1) * G, :].rearrange("(t p) d -> p t d", p=128), in_=h_tok[:]))(),
                 reads=[f"h_tok{t}" for t in range(NT)], writes=["y_out"], dma_key="yout")
        S.op("sp", lambda h: None, reads=["y_out", "dbg_out"])

        keys = S.plan()
        sems = {k: st.enter_context(nc.semaphore(k)) for k in keys}
        if dbg:
            print("sems", len(keys), "waits", S.nwaits, {e: len(S.ops[e]) for e in ENGS}, S.counts)
        with nc.Block() as block:
            @block.tensor
            def _(h):
                S.emit_engine("pe", h, sems)

            @block.vector
            def _(h):
                S.emit_engine("dve", h, sems)

            @block.scalar
            def _(h):
                S.emit_engine("act", h, sems)

            @block.gpsimd
            def _(h):
                S.emit_engine("pool", h, sems)

            @block.sync
            def _(h):
                S.emit_engine("sp", h, sems)
    return nc


def kernel(**inputs):
    shared = prep_shared(inputs)
    x = np.asarray(inputs["x"], np.float32)
    p = np.asarray(inputs["p"], np.float32)
    nc = build_nc(SEQ // G, (0, 1))
    in_maps = []
    for c in range(NB):
        m = dict(shared)
        m["x"] = np.ascontiguousarray(x[c])
        m["p"] = np.ascontiguousarray(p[:, c])
        in_maps.append(m)
    res = run_bass_kernel_spmd(nc, in_maps, core_ids=list(range(NB)))
    return np.stack([r["y"] for r in res.results], axis=0).astype(np.float32)
```
